# Optimizing a Trainium2 kernel written in Bass

```python
import jax, jax.numpy as jnp
from jax import lax
import numpy as np

D_MODEL = 1024
BATCH = 1
SEQ = 16384
DEPTH = 1

N_HEADS = 16
HEAD_DIM = 64
N_KV_GROUPS = 2
HEADS_PER_GROUP = N_HEADS // N_KV_GROUPS
CMP_BLOCK = 32
CMP_STRIDE = 16
CMP_HIDDEN = 256
SLC_BLOCK = 64
N_SELECT = 16
WINDOW = 512
Q_BLOCK = 128
CONV_WIDTH = 1024
CONV_K = 3
D_FF = 2816
EPS = 1e-6
NEG_INF = -1e30
FORCE_SCORE = 1e9

Q_COLS = N_HEADS * HEAD_DIM
KV_COLS = N_KV_GROUPS * HEAD_DIM
IN_WIDTHS = (Q_COLS, KV_COLS, KV_COLS, KV_COLS, KV_COLS, KV_COLS, KV_COLS, N_HEADS * 3,
             CONV_WIDTH, CONV_WIDTH, CONV_WIDTH, D_MODEL, D_MODEL)
IN_TOTAL = sum(IN_WIDTHS)

kernel_name = "hybrid_nsa_shortconv_macaron"


def rms_norm(x, g):
    xf = x.astype(jnp.float32)
    y = xf * lax.rsqrt(jnp.mean(xf * xf, axis=-1, keepdims=True) + EPS)
    return (y * g.astype(jnp.float32)).astype(x.dtype)


def swiglu(x, w_gate, w_up, w_down):
    return (jax.nn.silu(x @ w_gate) * (x @ w_up)) @ w_down


def compress_blocks(kv, pe, w1, w2):
    b, s, g, d = kv.shape
    chunks = kv.reshape(b, s // CMP_STRIDE, CMP_STRIDE, g, d)
    blocks = jnp.concatenate([chunks[:, :-1], chunks[:, 1:]], axis=2)
    blocks = blocks + pe[None, None, :, None, :]
    n_cmp = blocks.shape[1]
    flat = blocks.transpose(0, 1, 3, 2, 4).reshape(b, n_cmp, g, CMP_BLOCK * d)
    return jax.nn.gelu(flat @ w1) @ w2


def nsa_attention(q, kc, vc, k_slc, v_slc, k_win, v_win):
    b, s, h, d = q.shape
    g = N_KV_GROUPS
    scale = d ** -0.5
    n_cmp = kc.shape[1]
    n_slc = s // SLC_BLOCK
    n_sel = min(N_SELECT, n_slc)
    cmp_start = jnp.arange(n_cmp) * CMP_STRIDE
    cmp_end = cmp_start + CMP_BLOCK - 1
    slc_start = jnp.arange(n_slc) * SLC_BLOCK
    overlap = ((cmp_start[:, None] < slc_start[None, :] + SLC_BLOCK)
               & (cmp_start[:, None] + CMP_BLOCK > slc_start[None, :])).astype(jnp.float32)
    ks_blocks = k_slc.reshape(b, n_slc, SLC_BLOCK, g, d).transpose(0, 3, 1, 2, 4)
    vs_blocks = v_slc.reshape(b, n_slc, SLC_BLOCK, g, d).transpose(0, 3, 1, 2, 4)
    kw = jnp.pad(k_win, ((0, 0), (WINDOW, 0), (0, 0), (0, 0)))
    vw = jnp.pad(v_win, ((0, 0), (WINDOW, 0), (0, 0), (0, 0)))
    qg = q.reshape(b, s, g, HEADS_PER_GROUP, d)
    b_idx = jnp.arange(b)[:, None, None, None]
    g_idx = jnp.arange(g)[None, :, None, None]
    blk = jnp.arange(n_slc)

    def query_block(i):
        q0 = i * Q_BLOCK
        qb = lax.dynamic_slice_in_dim(qg, q0, Q_BLOCK, axis=1)
        t = q0 + jnp.arange(Q_BLOCK)
        sc = jnp.einsum('bqghd,bcgd->bghqc', qb, kc).astype(jnp.float32) * scale
        valid_c = cmp_end[None, :] <= t[:, None]
        pc = jax.nn.softmax(jnp.where(valid_c, sc, NEG_INF), axis=-1)
        pc = jnp.where(valid_c, pc, 0.0)
        o_cmp = jnp.einsum('bghqc,bcgd->bqghd', pc.astype(vc.dtype), vc)
        ps = jnp.einsum('bghqc,cn->bgqn', pc, overlap)
        cur = t // SLC_BLOCK
        forced = (blk[None, :] == 0) | (blk[None, :] == cur[:, None]) | (blk[None, :] == cur[:, None] - 1)
        causal_blk = blk[None, :] * SLC_BLOCK <= t[:, None]
        ps = jnp.where(forced, FORCE_SCORE, jnp.where(causal_blk, ps, -1.0))
        _, idx = lax.top_k(ps, n_sel)
        ksel = ks_blocks[b_idx, g_idx, idx].reshape(b, g, Q_BLOCK, n_sel * SLC_BLOCK, d)
        vsel = vs_blocks[b_idx, g_idx, idx].reshape(b, g, Q_BLOCK, n_sel * SLC_BLOCK, d)
        pos = (idx[..., None] * SLC_BLOCK + jnp.arange(SLC_BLOCK)).reshape(b, g, Q_BLOCK, n_sel * SLC_BLOCK)
        valid_s = (pos <= t[:, None])[:, :, None]
        ss = jnp.einsum('bqghd,bgqkd->bghqk', qb, ksel).astype(jnp.float32) * scale
        psel = jax.nn.softmax(jnp.where(valid_s, ss, NEG_INF), axis=-1)
        o_slc = jnp.einsum('bghqk,bgqkd->bqghd', psel.astype(vsel.dtype), vsel)
        kwb = lax.dynamic_slice_in_dim(kw, q0, WINDOW + Q_BLOCK, axis=1)
        vwb = lax.dynamic_slice_in_dim(vw, q0, WINDOW + Q_BLOCK, axis=1)
        wpos = q0 - WINDOW + jnp.arange(WINDOW + Q_BLOCK)
        diff = t[:, None] - wpos[None, :]
        valid_w = (diff >= 0) & (diff < WINDOW) & (wpos[None, :] >= 0)
        sw = jnp.einsum('bqghd,bkgd->bghqk', qb, kwb).astype(jnp.float32) * scale
        pw = jax.nn.softmax(jnp.where(valid_w, sw, NEG_INF), axis=-1)
        o_win = jnp.einsum('bghqk,bkgd->bqghd', pw.astype(vwb.dtype), vwb)
        return o_cmp, o_slc, o_win

    o_cmp, o_slc, o_win = lax.map(query_block, jnp.arange(s // Q_BLOCK))

    def unblock(o):
        return o.transpose(1, 0, 2, 3, 4, 5).reshape(b, s, h, d)

    return unblock(o_cmp), unblock(o_slc), unblock(o_win)


def short_gated_conv(b_gate, c_gate, x_in, conv_w):
    u = c_gate * x_in
    conv = lax.conv_general_dilated(u, conv_w[:, None, :], window_strides=(1,),
                                    padding=[(CONV_K - 1, 0)],
                                    dimension_numbers=('NWC', 'WIO', 'NWC'),
                                    feature_group_count=CONV_WIDTH)
    return b_gate * conv


def setup_inputs(seed: int = 0) -> dict:
    key = jax.random.key(seed)
    ks = jax.random.split(key, 24)
    L = DEPTH

    def w(k, shape, fan_in):
        return jax.random.normal(k, shape, jnp.float32) * fan_in ** -0.5

    def gain(k):
        return 1.0 + 0.01 * jax.random.normal(k, (L, D_MODEL), jnp.float32)

    flat_cmp = CMP_BLOCK * HEAD_DIM
    return {
        "x": jax.random.normal(ks[0], (BATCH, SEQ, D_MODEL), jnp.float32),
        "ffn1_norm": gain(ks[1]),
        "ffn1_w_gate": w(ks[2], (L, D_MODEL, D_FF), D_MODEL),
        "ffn1_w_up": w(ks[3], (L, D_MODEL, D_FF), D_MODEL),
        "ffn1_w_down": w(ks[4], (L, D_FF, D_MODEL), D_FF),
        "mix_norm": gain(ks[5]),
        "w_in": w(ks[6], (L, D_MODEL, IN_TOTAL), D_MODEL),
        "cmp_pe_k": 0.02 * jax.random.normal(ks[7], (L, CMP_BLOCK, HEAD_DIM), jnp.float32),
        "cmp_pe_v": 0.02 * jax.random.normal(ks[8], (L, CMP_BLOCK, HEAD_DIM), jnp.float32),
        "cmp_k_w1": w(ks[9], (L, flat_cmp, CMP_HIDDEN), flat_cmp),
        "cmp_k_w2": w(ks[10], (L, CMP_HIDDEN, HEAD_DIM), CMP_HIDDEN),
        "cmp_v_w1": w(ks[11], (L, flat_cmp, CMP_HIDDEN), flat_cmp),
        "cmp_v_w2": w(ks[12], (L, CMP_HIDDEN, HEAD_DIM), CMP_HIDDEN),
        "conv_w": w(ks[13], (L, CONV_K, CONV_WIDTH), CONV_K),
        "w_nsa_out": w(ks[14], (L, N_HEADS * HEAD_DIM, D_MODEL), N_HEADS * HEAD_DIM),
        "w_conv_out": w(ks[15], (L, CONV_WIDTH, D_MODEL), CONV_WIDTH),
        "w_out": w(ks[16], (L, D_MODEL, D_MODEL), D_MODEL),
        "ffn2_norm": gain(ks[17]),
        "ffn2_w_gate": w(ks[18], (L, D_MODEL, D_FF), D_MODEL),
        "ffn2_w_up": w(ks[19], (L, D_MODEL, D_FF), D_MODEL),
        "ffn2_w_down": w(ks[20], (L, D_FF, D_MODEL), D_FF),
        "final_norm": 1.0 + 0.01 * jax.random.normal(ks[21], (D_MODEL,), jnp.float32),
    }


def reference(x, ffn1_norm, ffn1_w_gate, ffn1_w_up, ffn1_w_down, mix_norm, w_in,
              cmp_pe_k, cmp_pe_v, cmp_k_w1, cmp_k_w2, cmp_v_w1, cmp_v_w2, conv_w,
              w_nsa_out, w_conv_out, w_out, ffn2_norm, ffn2_w_gate, ffn2_w_up,
              ffn2_w_down, final_norm):
    b, s, _ = x.shape
    split_at = [int(v) for v in np.cumsum(IN_WIDTHS)[:-1]]
    for l in range(DEPTH):
        x = x + 0.5 * swiglu(rms_norm(x, ffn1_norm[l]), ffn1_w_gate[l], ffn1_w_up[l], ffn1_w_down[l])
        h = rms_norm(x, mix_norm[l])
        proj = h @ w_in[l]
        (q, k_c, v_c, k_s, v_s, k_w, v_w, nsa_g,
         conv_b, conv_c, conv_x, gate_a, gate_b) = jnp.split(proj, split_at, axis=-1)
        q = q.reshape(b, s, N_HEADS, HEAD_DIM)
        kv_shape = (b, s, N_KV_GROUPS, HEAD_DIM)
        kc = compress_blocks(k_c.reshape(kv_shape), cmp_pe_k[l], cmp_k_w1[l], cmp_k_w2[l])
        vc = compress_blocks(v_c.reshape(kv_shape), cmp_pe_v[l], cmp_v_w1[l], cmp_v_w2[l])
        o_cmp, o_slc, o_win = nsa_attention(q, kc, vc, k_s.reshape(kv_shape), v_s.reshape(kv_shape),
                                            k_w.reshape(kv_shape), v_w.reshape(kv_shape))
        g3 = jax.nn.sigmoid(nsa_g.reshape(b, s, N_HEADS, 3))
        o_nsa = g3[..., 0:1] * o_cmp + g3[..., 1:2] * o_slc + g3[..., 2:3] * o_win
        y_a = o_nsa.reshape(b, s, N_HEADS * HEAD_DIM) @ w_nsa_out[l]
        y_b = short_gated_conv(conv_b, conv_c, conv_x, conv_w[l]) @ w_conv_out[l]
        merged = jax.nn.sigmoid(gate_a) * y_a + jax.nn.sigmoid(gate_b) * y_b
        x = x + merged @ w_out[l]
        x = x + 0.5 * swiglu(rms_norm(x, ffn2_norm[l]), ffn2_w_gate[l], ffn2_w_up[l], ffn2_w_down[l])
    return rms_norm(x, final_norm)
```

```python
import numpy as np
from contextlib import ExitStack
import concourse.bass as bass
import concourse.mybir as mybir
from concourse.bass_utils import run_bass_kernel_spmd

F32 = mybir.dt.float32
BF16 = mybir.dt.bfloat16
AF = mybir.ActivationFunctionType
ALU = mybir.AluOpType

NCORES = 8
D = 1024
DFF = 2816
NFC = DFF // 128
SEQ = 16384
NLB = 16
NTL = NLB * 128
NHALO = 32
EPS = 1e-6
NEG = -30000.0


class Sched:
    EPOCH = 12000

    def __init__(self, nc, es):
        self.nc = nc
        self.es = es
        self.eng = dict(pe=nc.tensor, act=nc.scalar, dve=nc.vector, pool=nc.gpsimd, sp=nc.sync)
        self.cnt = {e: 0 for e in self.eng}
        self.esem = {e: [] for e in self.eng}
        self.seen = {e: {} for e in self.eng}
        self.res = {}
        self.dsem = {}
        self.nsem = 0
        self.nwait = 0
        self.free = []

    def _newsem(self, name):
        self.nsem += 1
        return self.es.enter_context(self.nc.semaphore(name))

    def _esem(self, e, epoch):
        while len(self.esem[e]) <= epoch:
            self.esem[e].append(self._newsem(f"e_{e}_{len(self.esem[e])}"))
        return self.esem[e][epoch]

    def _wait(self, e, tok, same_ok=True):
        if tok[0] == 'eng':
            _, f, seq = tok
            if f == e and same_ok and e == 'pe':
                return
            if self.seen[e].get(f, 0) >= seq:
                return
            self.seen[e][f] = seq
            ep = (seq - 1) // self.EPOCH
            self.eng[e].wait_ge(self._esem(f, ep), seq - ep * self.EPOCH)
            self.nwait += 1
        else:
            _, key, val = tok
            sem, cur = self.dsem[key]
            if self.seen[e].get(('d', key), 0) >= val:
                return
            self.seen[e][('d', key)] = cur
            self.eng[e].wait_ge(sem, cur)
            self.nwait += 1

    def _deps(self, reads, writes):
        deps = []
        for r in reads:
            R = self.res.get(r)
            if R and R['w']:
                deps.append(R['w'])
            if R and r[0] == 'ps':
                deps.extend(R['r'].values())
        for w in writes:
            R = self.res.get(w)
            if R:
                if R['w']:
                    deps.append(R['w'])
                deps.extend(R['r'].values())
        return deps

    def _record(self, tok, reads, writes):
        for r in reads:
            R = self.res.setdefault(r, {'w': None, 'r': {}})
            R['r'][(tok[0], tok[1])] = tok
        for w in writes:
            self.res[w] = {'w': tok, 'r': {}}

    def op(self, e, fn, reads=(), writes=()):
        for d in self._deps(reads, writes):
            self._wait(e, d)
        inst = fn(self.eng[e])
        self.cnt[e] += 1
        seq = self.cnt[e]
        ep = (seq - 1) // self.EPOCH
        inst.then_inc(self._esem(e, ep), 1)
        self._record(('eng', e, seq), reads, writes)

    def dma(self, q, out, in_, reads=(), writes=(), key=None, **kw):
        for d in self._deps(reads, writes):
            self._wait(q, d, same_ok=False)
        if key not in self.dsem:
            if self.free:
                self.dsem[key] = self.free.pop()
            else:
                self.dsem[key] = [self._newsem("d_%d" % self.nsem), 0]
        ent = self.dsem[key]
        self.eng[q].dma_start(out=out, in_=in_, **kw).then_inc(ent[0], 16)
        ent[1] += 16
        self._record(('dma', key, ent[1]), reads, writes)

    def raw(self, e, fn, reads=(), writes=(), inc=1):
        for d in self._deps(reads, writes):
            self._wait(e, d, same_ok=False)
        key = ('raw', self.nsem)
        sem = self._newsem("r_%d" % self.nsem)
        fn(self.eng[e]).then_inc(sem, inc) if inc != 1 else fn(self.eng[e]).then_inc(sem)
        self.dsem[key] = [sem, inc]
        self._record(('dma', key, inc), reads, writes)

    def barrier(self):
        for e in self.eng:
            for f in self.eng:
                if f != e and self.cnt[f] > 0:
                    self._wait(e, ('eng', f, self.cnt[f]))
            for key, (sem, cur) in self.dsem.items():
                if cur > 0:
                    self._wait(e, ('dma', key, cur))
        self.res = {}
        for key in list(self.dsem.keys()):
            if key[0] != 'raw':
                self.free.append(self.dsem.pop(key))

    def final_wait(self, e='sp'):
        for f in self.eng:
            if f != e and self.cnt[f] > 0:
                self._wait(e, ('eng', f, self.cnt[f]))
        for key, (sem, cur) in self.dsem.items():
            if cur > 0:
                self._wait(e, ('dma', key, cur))


def mm(S, out, lhsT, rhs, start, stop, reads, writes, **kw):
    S.op('pe', lambda e: e.matmul(out, lhsT, rhs, start=start, stop=stop, **kw), reads=reads, writes=writes)


class Ctx:
    pass


def rms_prologue(S, C, src_ap, nb, slot, xn_tile, xn_key, src_reads, tag):
    ssq_col = C.ssq[0:nb, slot * 16:slot * 16 + 1]
    sq_col = C.sq[0:nb, slot * 16:slot * 16 + 1]
    rstd_col = C.rstd[0:nb, slot * 16:slot * 16 + 1]
    S.op('act', lambda e: e.activation(out=C.junk[0:nb, :], in_=src_ap, func=AF.Square, accum_out=ssq_col),
         reads=src_reads, writes=[('junk',), (tag, 'ssq')])
    S.op('act', lambda e: e.activation(out=sq_col, in_=ssq_col, func=AF.Sqrt, bias=C.eps_t[0:nb, 0:1], scale=1.0 / D),
         reads=[(tag, 'ssq'), (tag, 'rstd')], writes=[(tag, 'rstd0')])
    S.op('dve', lambda e: e.reciprocal(out=rstd_col, in_=sq_col), reads=[(tag, 'rstd0')], writes=[(tag, 'rstd')])
    S.op('dve', lambda e: e.tensor_scalar(out=xn_tile[0:nb, :], in0=src_ap, scalar1=rstd_col, scalar2=None, op0=ALU.mult),
         reads=list(src_reads) + [(tag, 'rstd')], writes=[xn_key])


def transpose_unit(S, C, xn_tile, xn_key, nb, ps_idx, dst_ap, dst_key, gbc):
    pst = C.PS[ps_idx][:].bitcast(BF16)
    for kc in range(8):
        S.op('pe', lambda e, kc=kc: e.transpose(out=pst[:, kc * 128:kc * 128 + nb], in_=xn_tile[0:nb, kc * 128:(kc + 1) * 128],
                                               identity=C.ident[0:nb, 0:nb]),
             reads=[xn_key], writes=[('ps', ps_idx)])
    src = pst.rearrange("p (k t) -> p k t", k=8)[:, :, 0:nb]
    S.op('dve', lambda e: e.tensor_tensor(out=dst_ap, in0=src, in1=gbc[:, :, 0:nb], op=ALU.mult),
         reads=[('ps', ps_idx)], writes=[dst_key])


def ffn_phase(nc, S, C, tag, x_d, units, g_d, wg_d, wu_d, wd_d, epilogue):
    with ExitStack() as es:
        def sb(name, shape, dt):
            return es.enter_context(nc.sbuf_tensor(f"{tag}_{name}", shape, dt))
        P = Ctx()
        P.wd = sb("wd", [128, NFC, 1024], BF16)
        P.xs = sb("xs", [128, 9, 1024], F32)
        P.xnT = sb("xnT", [128, 8, 1056], BF16)
        P.H = sb("H", [128, NFC, 1056], BF16)
        P.wg = [sb(f"wg{i}", [128, 8, 128], BF16) for i in range(3)]
        P.wu = [sb(f"wu{i}", [128, 8, 128], BF16) for i in range(3)]
        P.xn = [sb(f"xn{i}", [128, 1024], BF16) for i in range(2)]
        P.sg = [sb(f"sg{i}", [128, 512], BF16) for i in range(2)]
        P.g = sb("g", [128, 8], F32)
        P.gbc = sb("gbc", [128, 8, 128], F32)
        P.ep = es
        epi_state = epilogue(None, None, P, None, None, None, None, init=True, sb=sb)

        S.dma('sp', P.g[:], g_d[:, :], writes=[(tag, 'g')], key=(tag, 'g'))
        S.op('dve', lambda e: e.tensor_copy(out=P.gbc[:], in_=P.g[:].unsqueeze(2).broadcast_to([128, 8, 128])),
             reads=[(tag, 'g')], writes=[(tag, 'gbc')])
        for fc in range(NFC):
            S.dma('pool', P.wd[:, fc, :], wd_d[fc * 128:(fc + 1) * 128, :], writes=[(tag, 'wd', fc)], key=(tag, 'wd', fc % 2))

        npass = 2
        per = len([u for u in units if u[1] == 128]) // npass
        wcount = 0
        for ps in range(npass):
            pun = [(i, u) for i, u in enumerate(units) if u[1] == 128][ps * per:(ps + 1) * per]
            if ps == npass - 1:
                pun += [(i, u) for i, u in enumerate(units) if u[1] != 128]
            cols = []
            c0 = 0
            for (ui, (r0, nb)) in pun:
                cols.append(c0)
                c0 += nb
            ncols = c0
            for si, (ui, (r0, nb)) in enumerate(pun):
                S.dma('sp', P.xs[0:nb, si, :], x_d[r0:r0 + nb, :], writes=[(tag, 'xs', si)], key=(tag, 'xs', si))
                xnk = (tag, 'xn', si % 2)
                rms_prologue(S, C, P.xs[0:nb, si, :], nb, si, P.xn[si % 2], xnk,
                             [(tag, 'xs', si)], (tag, 'pro', si))
                transpose_unit(S, C, P.xn[si % 2], xnk, nb, 6 + si % 2, P.xnT[:, :, cols[si]:cols[si] + nb], (tag, 'xnT', si), P.gbc)
            subt = []
            c = 0
            while c < ncols:
                n = min(512, ncols - c) if (ncols - c) >= 512 else ncols - c
                subt.append((c, n))
                c += n
            xnT_keys = [(tag, 'xnT', si) for si in range(len(pun))]
            ev = 0
            for fc in range(NFC):
                wb = wcount % 3
                wcount += 1
                S.dma('pool', P.wg[wb][:], wg_d[fc], reads=[], writes=[(tag, 'wg', wb)], key=(tag, 'wg', wb))
                S.dma('pool', P.wu[wb][:], wu_d[fc], reads=[], writes=[(tag, 'wu', wb)], key=(tag, 'wu', wb))
                for (c, n) in subt:
                    pb = ev % 2
                    ev += 1
                    pg = C.PS[0 + pb]
                    pu = C.PS[2 + pb]
                    for kc in range(8):
                        mm(S, pg[:, 0:n], P.wg[wb][:, kc, :], P.xnT[:, kc, c:c + n], kc == 0, kc == 7,
                           reads=[(tag, 'wg', wb)] + xnT_keys, writes=[('ps', 0 + pb)])
                    for kc in range(8):
                        mm(S, pu[:, 0:n], P.wu[wb][:, kc, :], P.xnT[:, kc, c:c + n], kc == 0, kc == 7,
                           reads=[(tag, 'wu', wb)] + xnT_keys, writes=[('ps', 2 + pb)])
                    S.op('act', lambda e: e.activation(out=P.sg[pb][:, 0:n], in_=pg[:, 0:n], func=AF.Silu),
                         reads=[('ps', 0 + pb)], writes=[(tag, 'sg', pb)])
                    S.op('dve', lambda e: e.tensor_tensor(out=P.H[:, fc, c:c + n], in0=P.sg[pb][:, 0:n], in1=pu[:, 0:n], op=ALU.mult),
                         reads=[(tag, 'sg', pb), ('ps', 2 + pb)], writes=[(tag, 'H', fc, c)])
            for si, (ui, (r0, nb)) in enumerate(pun):
                c = cols[si]
                hkeys = [(tag, 'H', fc, cc) for fc in range(NFC) for (cc, nn) in subt if cc <= c < cc + nn]
                pys = []
                for nh in range(2):
                    pi = 4 + nh
                    py = C.PS[pi]
                    for fc in range(NFC):
                        mm(S, py[0:nb, :], P.H[:, fc, c:c + nb], P.wd[:, fc, nh * 512:(nh + 1) * 512], fc == 0, fc == NFC - 1,
                           reads=[(tag, 'wd', fc), (tag, 'H', fc, [cc for (cc, nn) in subt if cc <= c < cc + nn][0])],
                           writes=[('ps', pi)])
                    pys.append(pi)
                epilogue(S, C, P, ui, (r0, nb), pys, (P.xs[0:nb, si, :], (tag, 'xs', si)), init=False, sb=None, state=epi_state)
        S.barrier()


class PSRot:
    def __init__(self, idxs):
        self.idxs = list(idxs)
        self.n = 0

    def next(self):
        i = self.idxs[self.n % len(self.idxs)]
        self.n += 1
        return i


TOK_TILES = [(0, 512), (512, 512), (1024, 512), (1536, 512), (2048, NHALO)]
CH_Q, CH_FM, CH_TM, CH_B, CH_C, CH_X, CH_GA, CH_GB = 0, 8, 12, 15, 23, 31, 39, 47
NWCH = 55


_uid = [0]


def load_hT(nc, S, es, Dm):
    _uid[0] += 1
    hT = es.enter_context(nc.sbuf_tensor("hT_all%d" % _uid[0], [128, 8, NTL + NHALO], BF16))
    for u in range(17):
        nb = 128 if u < 16 else NHALO
        S.dma('sp', hT[:, :, u * 128:u * 128 + nb], Dm['hT'][:, u, :, 0:nb], writes=[('hT', u)], key=('hTl', u % 4))
    return hT


def hT_keys(c0, n):
    return [('hT', u) for u in range(c0 // 128, (c0 + n - 1) // 128 + 1)]


def proj_fm(S, C, hT, w, wkey, c0, n, pi):
    for kc in range(8):
        mm(S, C.PS[pi][:, 0:n], w[:, kc, :], hT[:, kc, c0:c0 + n], kc == 0, kc == 7,
           reads=[wkey] + hT_keys(c0, n), writes=[('ps', pi)])


def phase_proj(nc, S, C, Dm):
    with ExitStack() as es:
        def sb(name, shape, dtp):
            return es.enter_context(nc.sbuf_tensor("pj_" + name, shape, dtp))
        hT = load_hT(nc, S, es, Dm)
        qT = sb("qT", [128, 8, NTL], BF16)
        fm = sb("fm", [128, 4, NTL], BF16)
        tm = sb("tm", [128, NLB, 256], BF16)
        wtm = sb("wtm", [128, 8, 384], BF16)
        wb = [sb(f"w{i}", [128, 8, 128], BF16) for i in range(3)]
        rot = PSRot(range(8))
        win = Dm['win']
        for j in range(3):
            S.dma('pool', wtm[:, :, j * 128:(j + 1) * 128], win[CH_TM + j], writes=[('wtm',)], key=('wtm',))
        nw = 0
        ev = 0
        for ch in list(range(CH_FM, CH_FM + 4)):
            k = nw % 3
            nw += 1
            S.dma('pool', wb[k][:], win[ch], writes=[('pjw', k)], key=('pjw', k))
            for (c0, n) in TOK_TILES[:4]:
                pi = rot.next()
                proj_fm(S, C, hT, wb[k], ('pjw', k), c0, n, pi)
                dst = fm[:, ch - CH_FM, c0:c0 + n]
                eng = 'act' if ev % 2 == 0 else 'dve'
                ev += 1
                if eng == 'act':
                    S.op('act', lambda e: e.activation(out=dst, in_=C.PS[pi][:, 0:n], func=AF.Copy), reads=[('ps', pi)], writes=[('fm', ch, c0)])
                else:
                    S.op('dve', lambda e: e.tensor_copy(out=dst, in_=C.PS[pi][:, 0:n]), reads=[('ps', pi)], writes=[('fm', ch, c0)])
        fmkeys = [('fm', ch, c0) for ch in range(CH_FM, CH_FM + 4) for (c0, n) in TOK_TILES[:4]]
        S.dma('sp', Dm['xb_in'][0:512, :].rearrange("(k p) t -> p k t", p=128), fm[:], reads=fmkeys, writes=[('xb_in', 'fm')], key=('xbst', 0))
        if 'pj1' in Dm['_debug']:
            S.barrier()
            return
        for u in range(NLB):
            pi = rot.next()
            for kc in range(8):
                mm(S, C.PS[pi][:, 0:304], hT[:, kc, u * 128:(u + 1) * 128], wtm[:, kc, 0:304], kc == 0, kc == 7,
                   reads=[('wtm',), ('hT', u)], writes=[('ps', pi)])
            S.op('act', lambda e: e.activation(out=tm[:, u, :], in_=C.PS[pi][:, 0:256], func=AF.Copy), reads=[('ps', pi)], writes=[('tm', u)])
            S.op('act', lambda e: e.activation(out=C.g3[:, u, :], in_=C.PS[pi][:, 256:304], func=AF.Sigmoid), reads=[('ps', pi)], writes=[('g3', u)])
        tmview = Dm['xb_in'][512:768, :].rearrange("a (b c) -> (a b) c", c=256).rearrange("(u p) c -> p u c", p=128)
        for u4 in range(4):
            S.dma('sp', tmview[:, u4 * 4:(u4 + 1) * 4, :], tm[:, u4 * 4:(u4 + 1) * 4, :], reads=[('tm', u) for u in range(u4 * 4, u4 * 4 + 4)],
                  writes=[('xb_in', 'tm', u4)], key=('xbst', 1))
        if 'pj2' in Dm['_debug']:
            S.barrier()
            return
        if 'no_cc' not in Dm['_debug']:
          S.raw('pool', lambda e: e.collective_compute("AllGather", ALU.bypass, replica_groups=[list(range(NCORES))],
                                                    ins=[Dm['xb_in_t'].ap().opt()], outs=[Dm['xg_t'].ap().opt()]),
                reads=[('xb_in', 'fm')] + [('xb_in', 'tm', u4) for u4 in range(4)], writes=[('xg',)])
        for ch in range(CH_Q, CH_Q + 8):
            k = nw % 3
            nw += 1
            S.dma('pool', wb[k][:], win[ch], writes=[('pjw', k)], key=('pjw', k))
            for (c0, n) in TOK_TILES[:4]:
                pi = rot.next()
                proj_fm(S, C, hT, wb[k], ('pjw', k), c0, n, pi)
                dst = qT[:, ch, c0:c0 + n]
                eng = 'act' if ev % 2 == 0 else 'dve'
                ev += 1
                if eng == 'act':
                    S.op('act', lambda e: e.activation(out=dst, in_=C.PS[pi][:, 0:n], func=AF.Copy), reads=[('ps', pi)], writes=[('qT', ch, c0)])
                else:
                    S.op('dve', lambda e: e.tensor_copy(out=dst, in_=C.PS[pi][:, 0:n]), reads=[('ps', pi)], writes=[('qT', ch, c0)])
        S.dma('sp', Dm['qT'][:, :, :], qT[:], reads=[('qT', ch, c0) for ch in range(8) for (c0, n) in TOK_TILES[:4]], writes=[('qT_d',)], key=('qTst',))
        S.barrier()


def phase_conv(nc, S, C, Dm):
    with ExitStack() as es:
        def sb(name, shape, dtp):
            return es.enter_context(nc.sbuf_tensor("cv_" + name, shape, dtp))
        hT = load_hT(nc, S, es, Dm)
        ZT = sb("ZT", [128, 8, NTL], BF16)
        upad = sb("upad", [128, NLB, 130], F32)
        csb = sb("csb", [128, NTL + NHALO], F32)
        tcv = sb("tcv", [128, NLB, 128], F32)
        cw = sb("cw", [128, 8, 3], F32)
        wco = sb("wco", [128, 8, 1024], BF16)
        wb = [sb(f"w{i}", [128, 8, 128], BF16) for i in range(4)]
        stg = [sb(f"stg{i}", [128, 512], BF16) for i in range(4)]
        sgb = [sb(f"sgb{i}", [128, 512], BF16) for i in range(2)]
        rot = PSRot(range(8))
        win = Dm['win']
        S.dma('sp', cw[:], Dm['convw'][:, :, :], writes=[('cw',)], key=('cw',))
        for oc in range(8):
            S.dma('pool', wco[:, :, oc * 128:(oc + 1) * 128], Dm['wco'][oc], writes=[('wco', oc)], key=('wco', oc % 2))
        nw = 0

        def loadw(ch):
            nonlocal nw
            k = nw % 4
            nw += 1
            S.dma('pool', wb[k][:], win[ch], writes=[('cvw', k)], key=('cvw', k))
            return k
        for ch in range(8):
            kC = loadw(CH_C + ch)
            kX = loadw(CH_X + ch)
            kB = loadw(CH_B + ch)
            for (c0, n) in TOK_TILES:
                pi = rot.next()
                proj_fm(S, C, hT, wb[kC], ('cvw', kC), c0, n, pi)
                S.op('act', lambda e: e.activation(out=csb[:, c0:c0 + n], in_=C.PS[pi][:, 0:n], func=AF.Copy),
                     reads=[('ps', pi)], writes=[('csb', c0)])
            for (c0, n) in TOK_TILES:
                pi = rot.next()
                proj_fm(S, C, hT, wb[kX], ('cvw', kX), c0, n, pi)
                if n == 512:
                    r0 = c0 // 128
                    outv = upad[:, r0:r0 + 4, 2:130]
                    in0 = csb[:, c0:c0 + n].rearrange("p (r t) -> p r t", t=128)
                    in1 = C.PS[pi][:, 0:n].rearrange("p (r t) -> p r t", t=128)
                else:
                    outv = upad[:, :, 0:2]
                    in0 = csb[:, c0:c0 + n].rearrange("p (r t) -> p r t", t=2)
                    in1 = C.PS[pi][:, 0:n].rearrange("p (r t) -> p r t", t=2)
                S.op('dve', lambda e: e.tensor_tensor(out=outv, in0=in0, in1=in1, op=ALU.mult),
                     reads=[('ps', pi), ('csb', c0)], writes=[('upad', c0)])
            ukeys = [('upad', c0) for (c0, n) in TOK_TILES]
            S.op('dve', lambda e: e.tensor_scalar(out=tcv[:], in0=upad[:, :, 0:128], scalar1=cw[:, ch, 0:1], scalar2=None, op0=ALU.mult),
                 reads=ukeys + [('cw',)], writes=[('tcv',)])
            S.op('dve', lambda e: e.scalar_tensor_tensor(out=tcv[:], in0=upad[:, :, 1:129], scalar=cw[:, ch, 1:2], in1=tcv[:], op0=ALU.mult, op1=ALU.add),
                 reads=ukeys + [('tcv',)], writes=[('tcv',)])
            S.op('dve', lambda e: e.scalar_tensor_tensor(out=tcv[:], in0=upad[:, :, 2:130], scalar=cw[:, ch, 2:3], in1=tcv[:], op0=ALU.mult, op1=ALU.add),
                 reads=ukeys + [('tcv',)], writes=[('tcv',)])
            for (c0, n) in TOK_TILES[:4]:
                pi = rot.next()
                proj_fm(S, C, hT, wb[kB], ('cvw', kB), c0, n, pi)
                S.op('dve', lambda e: e.tensor_tensor(out=ZT[:, ch, c0:c0 + n], in0=tcv[:].rearrange("p r t -> p (r t)")[:, c0:c0 + n],
                                                      in1=C.PS[pi][:, 0:n], op=ALU.mult),
                     reads=[('ps', pi), ('tcv',)], writes=[('ZT', ch, c0)])
        ns = 0
        for oc in range(8):
            kGB = loadw(CH_GB + oc)
            kGA = loadw(CH_GA + oc)
            for (c0, n) in TOK_TILES[:4]:
                pg = rot.next()
                proj_fm(S, C, hT, wb[kGB], ('cvw', kGB), c0, n, pg)
                sk = ns % 2
                S.op('act', lambda e: e.activation(out=sgb[sk][:, 0:n], in_=C.PS[pg][:, 0:n], func=AF.Sigmoid),
                     reads=[('ps', pg)], writes=[('sgb', sk)])
                py = rot.next()
                for kc in range(8):
                    mm(S, C.PS[py][:, 0:n], wco[:, kc, oc * 128:(oc + 1) * 128], ZT[:, kc, c0:c0 + n], kc == 0, kc == 7,
                       reads=[('wco', oc)] + [('ZT', kc, c0)], writes=[('ps', py)])
                k = ns % 4
                S.op('dve', lambda e: e.tensor_tensor(out=stg[k][:, 0:n], in0=sgb[sk][:, 0:n], in1=C.PS[py][:, 0:n], op=ALU.mult),
                     reads=[('ps', py), ('sgb', sk)], writes=[('stg', k)])
                S.dma('sp', Dm['mb'][:, oc, c0:c0 + n], stg[k][:, 0:n], reads=[('stg', k)], writes=[], key=('stgst', k))
                ns += 1
                pa = rot.next()
                proj_fm(S, C, hT, wb[kGA], ('cvw', kGA), c0, n, pa)
                k = ns % 4
                S.op('act', lambda e: e.activation(out=stg[k][:, 0:n], in_=C.PS[pa][:, 0:n], func=AF.Sigmoid),
                     reads=[('ps', pa)], writes=[('stg', k)])
                S.dma('sp', Dm['sga'][:, oc, c0:c0 + n], stg[k][:, 0:n], reads=[('stg', k)], writes=[], key=('stgst', k))
                ns += 1
        S.barrier()


def phase_cmp(nc, S, C, Dm, A):
    with ExitStack() as es:
        def sb(name, shape, dtp):
            return es.enter_context(nc.sbuf_tensor("cm_" + name, shape, dtp))
        KC = [sb(f"kc2s{i}", [128, 17408], BF16) for i in range(2)]
        w1 = [sb(f"w1_{k}", [128, 16, 256], BF16) for k in range(2)]
        w2k = sb("w2k", [128, 2, 2, 128], BF16)
        w2v = sb("w2v", [128, 2, 64], BF16)
        pe = sb("pe", [128, 2, 16], BF16)
        bias = sb("bias", [128, 2, 2, 16], F32)
        G = sb("G", [128, 2, 2, 2, 1024], BF16)
        xs_ = [sb(f"gx{i}", [128, 512], F32) for i in range(2)]
        t1 = [sb(f"gt{i}", [128, 512], F32) for i in range(2)]
        sg = [sb(f"gs{i}", [128, 512], F32) for i in range(2)]
        rot = PSRot(range(8))
        xg = Dm['xg']
        for k in range(2):
            S.dma('pool', w1[k][:], Dm['w1'][k], writes=[('w1', k)], key=('w1', k))
            S.dma('pool', pe[:, k, :], Dm['pecol'][k], writes=[('pe', k)], key=('pe', k))
        for g in range(2):
            S.dma('pool', w2k[:, g, :, :], Dm['w2kpad'][g], writes=[('w2k', g)], key=('w2k', g))
        S.dma('pool', w2v[:], Dm['w2v'][:, :, :], writes=[('w2v',)], key=('w2v',))
        for k in range(2):
            for hh in range(2):
                pi = rot.next()
                for j in range(16):
                    mm(S, C.PS[pi][:, 0:1], w1[k][:, j, hh * 128:(hh + 1) * 128], pe[:, k, j:j + 1], j == 0, j == 15,
                       reads=[('w1', k), ('pe', k)], writes=[('ps', pi)])
                S.op('dve', lambda e: e.tensor_copy(out=bias[:, k, hh, 0:1], in_=C.PS[pi][:, 0:1]), reads=[('ps', pi)], writes=[('bias', k, hh)])
        n = 0
        for k in range(2):
            for g in range(2):
                b = n % 2
                n += 1
                kc = KC[b]
                S.op('pool', lambda e: e.memset(kc[:, 0:1], 0.0), writes=[('KC', b, 'z0')])
                S.op('pool', lambda e: e.memset(kc[:, 16385:16448], 0.0), writes=[('KC', b, 'z1')])
                S.op('pool', lambda e: e.memset(kc[64:128, 16384:16385], 0.0), writes=[('KC', b, 'z2')])
                for cc in range(NCORES):
                    row0 = cc * 768 + k * 128 + g * 64
                    src = xg[row0:row0 + 64, :].rearrange("d (r p) -> d r p", p=128)
                    lo = kc[0:64, 1 + cc * 128:1 + cc * 128 + 16 * 1024].rearrange("d (r q) -> d r q", q=1024)[:, :, 0:128]
                    hi = kc[64:128, cc * 128:cc * 128 + 16 * 1024].rearrange("d (r q) -> d r q", q=1024)[:, :, 0:128]
                    S.dma('sp', lo, src, reads=[('xg',)], writes=[('KC', b, 'lo', cc)], key=('KCl', b))
                    S.dma('sp', hi, src, reads=[('xg',)], writes=[('KC', b, 'hi', cc)], key=('KCl', b))
                kckeys = [('KC', b, 'z0'), ('KC', b, 'z1'), ('KC', b, 'z2')] + [('KC', b, h_, cc) for h_ in ('lo', 'hi') for cc in range(NCORES)]
                for hh in range(2):
                    for nt in range(2):
                        pi = rot.next()
                        for j in range(16):
                            st = 1 + 16 * (nt * 512 + (1 if j >= 8 else 0)) + 2 * (j % 8)
                            mm(S, C.PS[pi][:, :], w1[k][:, j, hh * 128:(hh + 1) * 128], kc[:, st:st + 16 * 512:16], j == 0, j == 15,
                               reads=[('w1', k)] + kckeys, writes=[('ps', pi)])
                        q = (hh * 2 + nt) % 2
                        S.op('act', lambda e: e.activation(out=xs_[q][:], in_=C.PS[pi][:, :], func=AF.Identity, bias=bias[:, k, hh, 0:1]),
                             reads=[('ps', pi), ('bias', k, hh)], writes=[('gx', q)])
                        S.op('dve', lambda e: e.tensor_tensor(out=t1[q][:], in0=xs_[q][:], in1=xs_[q][:], op=ALU.mult), reads=[('gx', q)], writes=[('gt', q)])
                        S.op('dve', lambda e: e.tensor_scalar(out=t1[q][:], in0=t1[q][:], scalar1=0.044715, scalar2=1.0, op0=ALU.mult, op1=ALU.add),
                             reads=[('gt', q)], writes=[('gt', q)])
                        S.op('dve', lambda e: e.tensor_tensor(out=t1[q][:], in0=t1[q][:], in1=xs_[q][:], op=ALU.mult), reads=[('gt', q), ('gx', q)], writes=[('gt', q)])
                        S.op('act', lambda e: e.activation(out=sg[q][:], in_=t1[q][:], func=AF.Sigmoid, scale=1.5957691216057308),
                             reads=[('gt', q)], writes=[('gs', q)])
                        S.op('dve', lambda e: e.tensor_tensor(out=G[:, k, g, hh, nt * 512:(nt + 1) * 512], in0=sg[q][:], in1=xs_[q][:], op=ALU.mult),
                             reads=[('gs', q), ('gx', q)], writes=[('G', k, g, hh, nt)])
        for nt in range(2):
            pi = rot.next()
            i = 0
            for g in range(2):
                for hh in range(2):
                    mm(S, C.PS[pi][:, :], w2k[:, g, hh, :], G[:, 0, g, hh, nt * 512:(nt + 1) * 512], i == 0, i == 3,
                       reads=[('w2k', g), ('G', 0, g, hh, nt)], writes=[('ps', pi)])
                    i += 1
            S.op('act', lambda e: e.activation(out=A.kcT[:, nt * 512:(nt + 1) * 512], in_=C.PS[pi][:, :], func=AF.Copy),
                 reads=[('ps', pi)], writes=[('kcT', nt)])
        S.op('pool', lambda e: e.memset(A.vc[:, :, :, 64:65], 1.0), writes=[('vc1',)])
        for T in range(8):
            pi = rot.next()
            for g in range(2):
                for hh in range(2):
                    mm(S, C.PS[pi][:, g * 64:(g + 1) * 64], G[:, 1, g, hh, T * 128:(T + 1) * 128], w2v[:, hh, :], hh == 0 and g == 0, hh == 1,
                       reads=[('w2v',), ('G', 1, g, hh, T // 4)], writes=[('ps', pi)], skip_group_check=True)
            S.op('dve', lambda e: e.tensor_copy(out=A.vc[:, T, :, 0:64], in_=C.PS[pi][:, 0:128].rearrange("p (g d) -> p g d", d=64)),
                 reads=[('ps', pi), ('vc1',)], writes=[('vc', T)])
        S.barrier()


def attn_branch(S, C, A, g, r, tiles, gate_col, first, tagb):
    nt = len(tiles)
    qv = A.qt[A.qslot][g * 64:(g + 1) * 64, :, :]

    def mask_group(i0):
        bank = 6 + (i0 // 8) % 2
        pst = C.PS[bank][:].bitcast(BF16)
        for j in range(i0, min(i0 + 8, nt)):
            src = tiles[j][1][1]
            S.op('pe', lambda e: e.transpose(out=pst[:, (j - i0) * 128:(j - i0 + 1) * 128], in_=src, identity=C.ident[:, :]),
                 reads=list(tiles[j][3]) + [('ident',)], writes=[('ps', bank)])

    steps = [(i, hh) for i in range(nt) for hh in range(2)]

    def stageA(sidx):
        i, hh = steps[sidx]
        kT, mk, va, rk = tiles[i]
        add = mk is not None and mk[0] == 'add'
        mul = mk is not None and mk[0] == 'mul'
        if mul and i % 8 == 0 and hh == 0:
            mask_group(i)
        pi = sidx % 4
        mm(S, C.PS[pi][:, :], kT, qv[:, hh * 4:(hh + 1) * 4, :], True, not add,
           reads=list(rk) + [('qt', A.qslot)], writes=[('ps', pi)])
        if add:
            mm(S, C.PS[pi][:, :], mk[1], A.I4[:, :], False, True, reads=list(rk) + [('I4',)], writes=[('ps', pi)])
        S.op('act', lambda e: e.activation(out=A.et[pi][:, :], in_=C.PS[pi][:, :], func=AF.Exp, scale=0.125),
             reads=[('ps', pi)], writes=[('et', pi)])
        if mul:
            bank = 6 + (i // 8) % 2
            mT = C.PS[bank][:].bitcast(BF16)[:, (i % 8) * 128:(i % 8 + 1) * 128]
            et3 = A.et[pi][:, :].rearrange("p (h q) -> p h q", q=128)
            S.op('dve', lambda e: e.tensor_tensor(out=et3, in0=et3, in1=mT.unsqueeze(1).broadcast_to([128, 4, 128]), op=ALU.mult),
                 reads=[('et', pi), ('ps', bank)], writes=[('et', pi)])

    def stageB(sidx):
        i, hh = steps[sidx]
        kT, mk, va, rk = tiles[i]
        pi = sidx % 4
        for h in range(4):
            mm(S, C.PS[4 + hh][:, h * 65:(h + 1) * 65], A.et[pi][:, h * 128:(h + 1) * 128], va, i == 0 and h == 0, i == nt - 1,
               reads=list(rk) + [('et', pi)], writes=[('ps', 4 + hh)], skip_group_check=True)
    ns = len(steps)
    LOOK = 3
    for sidx in range(min(LOOK, ns)):
        stageA(sidx)
    for sidx in range(ns):
        if sidx + LOOK < ns:
            stageA(sidx + LOOK)
        stageB(sidx)
    for hh in range(2):
        Ov = C.PS[4 + hh][:, 0:260].rearrange("p (h e) -> p h e", e=65)
        dk = (tagb, 'den', hh)
        S.op('dve', lambda e: e.tensor_scalar(out=A.den[:, hh * 16:hh * 16 + 4], in0=Ov[:, :, 64], scalar1=1e-30, scalar2=None, op0=ALU.add),
             reads=[('ps', 4 + hh)], writes=[dk])
        S.op('dve', lambda e: e.reciprocal(out=A.den[:, hh * 16:hh * 16 + 4], in_=A.den[:, hh * 16:hh * 16 + 4]), reads=[dk], writes=[dk])
        gsl = C.g3[:, r, :].rearrange("p (h b) -> p h b", b=3)[:, g * 8 + hh * 4:g * 8 + hh * 4 + 4, gate_col]
        S.op('dve', lambda e: e.tensor_tensor(out=A.den[:, hh * 16:hh * 16 + 4], in0=A.den[:, hh * 16:hh * 16 + 4], in1=gsl, op=ALU.mult),
             reads=[dk, ('g3l',)], writes=[dk])
        fac = A.den[:, hh * 16:hh * 16 + 4].unsqueeze(2).broadcast_to([128, 4, 64])
        oa = A.oacc[:, hh * 4:(hh + 1) * 4, :]
        if first:
            S.op('dve', lambda e: e.tensor_tensor(out=oa, in0=Ov[:, :, 0:64], in1=fac, op=ALU.mult),
                 reads=[('ps', 4 + hh), dk], writes=[('oacc', hh)])
        else:
            S.op('dve', lambda e: e.tensor_tensor(out=A.otmp[:, hh * 4:(hh + 1) * 4, :], in0=Ov[:, :, 0:64], in1=fac, op=ALU.mult),
                 reads=[('ps', 4 + hh), dk], writes=[('otmp', hh)])
            S.op('pool', lambda e: e.tensor_tensor(out=oa, in0=oa, in1=A.otmp[:, hh * 4:(hh + 1) * 4, :], op=ALU.add),
                 reads=[('otmp', hh), ('oacc', hh)], writes=[('oacc', hh)])


def phase_attn(nc, S, C, Dm, A):
    with ExitStack() as es:
        def sb(name, shape, dtp):
            return es.enter_context(nc.sbuf_tensor("at_" + name, shape, dtp))
        xg = Dm['xg']
        ksT = sb("ksT", [128, NCORES, NTL], BF16)
        vs = sb("vs", [128, NCORES, NLB, 2, 65], BF16)
        nme = sb("nme", [128, SEQ], BF16)
        A.qt = [sb(f"qt{i}", [128, 8, 128], BF16) for i in range(2)]
        kw = [sb(f"kw{i}", [128, 12, 128], BF16) for i in range(2)]
        vw = [sb(f"vw{i}", [128, 12, 2, 65], BF16) for i in range(2)]
        A.et = [sb(f"et{i}", [128, 512], BF16) for i in range(4)]
        A.I4 = sb("I4", [128, 512], BF16)
        A.den = sb("den", [128, 32], F32)
        A.oacc = sb("oacc", [128, 8, 64], F32)
        A.otmp = sb("otmp", [128, 8, 64], F32)
        obf = sb("obf", [128, 512], BF16)
        oT = [sb(f"oT{i}", [128, 4, 128], BF16) for i in range(2)]
        Eh = [sb(f"Eh{i}", [128, 1024], F32) for i in range(2)]
        PC = sb("PC", [128, 1040], F32)
        psm = sb("psm", [128, 256], F32)
        wrk = sb("wrk", [128, 256], F32)
        Msel = sb("Msel", [128, 256], F32)
        mx = sb("mx", [128, 16], F32)
        rs = sb("rs", [128, 64], F32)
        ri = sb("ri", [128, 16], F32)
        cmx = sb("cmx", [128, 256], BF16)
        tblm = sb("tblm", [128, 32], F32)
        tbla = sb("tbla", [128, 32], F32)
        cblk = sb("cblk", [128, 32], F32)
        cneg = sb("cneg", [128, 1024], BF16)
        negw = sb("negw", [128, 1536], BF16)
        S.dma('pool', cmx[:], Dm['cmaskx'][:, :], writes=[('cmx',)], key=('cst', 0))
        S.dma('pool', cneg[:], Dm['causalneg'][:, :], writes=[('cneg',)], key=('cst', 1))
        S.dma('pool', negw[:], Dm['negw'][:, :], writes=[('negw',)], key=('cst', 2))
        S.dma('sp', tblm[:], Dm['tblm'][:, :], writes=[('tblm',)], key=('cst', 3))
        S.dma('sp', tbla[:], Dm['tbla'][:, :], writes=[('tbla',)], key=('cst', 4))
        S.dma('sp', cblk[:], Dm['cblk'][:, :], writes=[('cblk',)], key=('cst', 5))
        for h in range(4):
            S.op('pool', lambda e: e.tensor_copy(out=A.I4[:, h * 128:(h + 1) * 128], in_=C.ident[:, :]), reads=[('ident',)], writes=[('I4',)])
        S.op('pool', lambda e: e.memset(PC[:, 0:1], 0.0), writes=[('PC0',)])
        S.op('pool', lambda e: e.memset(vs[:, :, :, :, 64:65], 1.0), writes=[('vs1',)])
        for i in range(2):
            S.op('pool', lambda e: e.memset(vw[i][:, :, :, 64:65], 1.0), writes=[('vw1', i)])
        for cc in range(NCORES):
            S.dma('sp', ksT[:, cc, :], xg[cc * 768 + 256:cc * 768 + 384, :], reads=[('xg',)], writes=[('ksT', cc)], key=('ksl', cc % 2))
            tmv = xg[cc * 768 + 512:cc * 768 + 768, :].rearrange("a (b c) -> (a b) c", c=256).rearrange("(u p) c -> p u c", p=128)
            for g in range(2):
                S.dma('sp', vs[:, cc, :, g, 0:64], tmv[:, :, g * 64:(g + 1) * 64], reads=[('xg',), ('vs1',)], writes=[('vs', cc, g)], key=('vsl', cc % 2))
        kv_keys = [('ksT', cc) for cc in range(NCORES)] + [('vs', cc, g_) for cc in range(NCORES) for g_ in range(2)]
        S.res[('g3l',)] = {'w': None, 'r': {}}
        nfin = 0
        for r in range(NLB):
            sl = r % 2
            A.qslot = sl
            S.dma('sp', A.qt[sl][:], Dm['qT'][:, :, r * 128:(r + 1) * 128], writes=[('qt', sl)], key=('qtl', sl))
            jj0 = 0 if r == 0 else -4
            wkeys = []
            if r > 0:
                srck = bass.AP(xg.tensor, ((4 * 768 + 384) * NTL + (r - 1) * 128), [[NTL, 128], [768 * NTL, 4], [1, 128]])
                S.dma('sp', kw[sl][:, 0:4, :], srck, reads=[('xg',)], writes=[('kw', sl, 0)], key=('kwl', sl))
                wkeys.append(('kw', sl, 0))
            srck = bass.AP(xg.tensor, (384 * NTL + r * 128), [[NTL, 128], [768 * NTL, 8], [1, 128]])
            S.dma('sp', kw[sl][:, 4:12, :], srck, reads=[('xg',)], writes=[('kw', sl, 1)], key=('kwl', sl))
            wkeys.append(('kw', sl, 1))
            for g in range(2):
                if r > 0:
                    srcv = bass.AP(xg.tensor, ((4 * 768 + 512) * NTL + ((r - 1) * 128) * 256 + 128 + g * 64), [[256, 128], [768 * NTL, 4], [1, 64]])
                    S.dma('sp', vw[sl][:, 0:4, g, 0:64], srcv, reads=[('xg',), ('vw1', sl)], writes=[('vw', sl, 0, g)], key=('vwl', sl))
                    wkeys.append(('vw', sl, 0, g))
                srcv = bass.AP(xg.tensor, (512 * NTL + (r * 128) * 256 + 128 + g * 64), [[256, 128], [768 * NTL, 8], [1, 64]])
                S.dma('sp', vw[sl][:, 4:12, g, 0:64], srcv, reads=[('xg',), ('vw1', sl)], writes=[('vw', sl, 1, g)], key=('vwl', sl))
                wkeys.append(('vw', sl, 1, g))
            nblk = 16 * (r + 1)
            n_c = 64 * (r + 1)
            for g in range(2):
                qv = A.qt[sl][g * 64:(g + 1) * 64, :, :]
                for h in range(8):
                    eb = h % 2
                    nbank = 1 if n_c <= 512 else 2
                    for bk in range(nbank):
                        w = min(512, n_c - bk * 512)
                        mm(S, C.PS[6 + bk][:, 0:w], qv[:, h, :], A.kcT[g * 64:(g + 1) * 64, bk * 512:bk * 512 + w], True, False,
                           reads=[('qt', sl), ('kcT', bk)], writes=[('ps', 6 + bk)])
                    pieces = [(n_c - 64, 128)] if r == 0 else [(n_c - 128, 64), (n_c - 64, 128)]
                    for (cst, mst) in pieces:
                        bk = cst // 512
                        mm(S, C.PS[6 + bk][:, cst - bk * 512:cst - bk * 512 + 64], C.ident[:, :], cmx[:, mst:mst + 64], False, True,
                           reads=[('ident',), ('cmx',)], writes=[('ps', 6 + bk)])
                    for bk in range(nbank):
                        w = min(512, n_c - bk * 512)
                        S.op('act', lambda e: e.activation(out=Eh[eb][:, bk * 512:bk * 512 + w], in_=C.PS[6 + bk][:, 0:w], func=AF.Exp, scale=0.125,
                                                           accum_out=rs[:, (h * 2 + bk) * 2:(h * 2 + bk) * 2 + 1]),
                             reads=[('ps', 6 + bk)], writes=[('Eh', eb, bk), ('rs', h, bk)])
                    rk = [('rs', h, 0)]
                    if nbank == 2:
                        S.op('dve', lambda e: e.tensor_tensor(out=ri[:, h:h + 1], in0=rs[:, h * 4:h * 4 + 1], in1=rs[:, h * 4 + 2:h * 4 + 3], op=ALU.add),
                             reads=[('rs', h, 0), ('rs', h, 1)], writes=[('ri', h)])
                        S.op('dve', lambda e: e.tensor_scalar(out=ri[:, h:h + 1], in0=ri[:, h:h + 1], scalar1=1e-30, scalar2=None, op0=ALU.add),
                             reads=[('ri', h)], writes=[('ri', h)])
                    else:
                        S.op('dve', lambda e: e.tensor_scalar(out=ri[:, h:h + 1], in0=rs[:, h * 4:h * 4 + 1], scalar1=1e-30, scalar2=None, op0=ALU.add),
                             reads=[('rs', h, 0)], writes=[('ri', h)])
                    S.op('dve', lambda e: e.reciprocal(out=ri[:, h:h + 1], in_=ri[:, h:h + 1]), reads=[('ri', h)], writes=[('ri', h)])
                    ehk = [('Eh', eb, bk) for bk in range(nbank)]
                    if h == 0:
                        S.op('dve', lambda e: e.tensor_scalar(out=PC[:, 1:1 + n_c], in0=Eh[eb][:, 0:n_c], scalar1=ri[:, h:h + 1], scalar2=None, op0=ALU.mult),
                             reads=ehk + [('ri', h)], writes=[('PC',)])
                    else:
                        S.op('dve', lambda e: e.scalar_tensor_tensor(out=PC[:, 1:1 + n_c], in0=Eh[eb][:, 0:n_c], scalar=ri[:, h:h + 1], in1=PC[:, 1:1 + n_c],
                                                                     op0=ALU.mult, op1=ALU.add),
                             reads=ehk + [('ri', h), ('PC',)], writes=[('PC',)])
                S.op('pool', lambda e: e.tensor_tensor(out=psm[:, 0:nblk], in0=PC[:, 0:4 * nblk:4], in1=PC[:, 1:1 + 4 * nblk:4], op=ALU.add),
                     reads=[('PC',), ('PC0',)], writes=[('psm',)])
                for k in range(2, 5):
                    S.op('pool', lambda e: e.tensor_tensor(out=psm[:, 0:nblk], in0=psm[:, 0:nblk], in1=PC[:, k:k + 4 * nblk:4], op=ALU.add),
                         reads=[('PC',), ('psm',)], writes=[('psm',)])
                nt_ = 16 if r == 0 else 32
                tb0 = 32 - nt_
                tail = psm[:, nblk - nt_:nblk]
                S.op('dve', lambda e: e.tensor_tensor(out=tail, in0=tail, in1=tblm[:, tb0:32], op=ALU.mult), reads=[('psm',), ('tblm',)], writes=[('psm',)])
                S.op('dve', lambda e: e.tensor_tensor(out=tail, in0=tail, in1=tbla[:, tb0:32], op=ALU.add), reads=[('psm',), ('tbla',)], writes=[('psm',)])
                S.op('dve', lambda e: e.memset(psm[:, 0:1], 1e9), reads=[('psm',)], writes=[('psm',)])
                S.op('dve', lambda e: e.max(out=mx[:, 0:8], in_=psm[:, 0:nblk]), reads=[('psm',)], writes=[('mx', 0)])
                S.op('dve', lambda e: e.match_replace(out=wrk[:, 0:nblk], in_to_replace=mx[:, 0:8], in_values=psm[:, 0:nblk], imm_value=-3.0),
                     reads=[('psm',), ('mx', 0)], writes=[('wrk',)])
                S.op('dve', lambda e: e.max(out=mx[:, 8:16], in_=wrk[:, 0:nblk]), reads=[('wrk',)], writes=[('mx', 1)])
                S.op('dve', lambda e: e.tensor_scalar(out=Msel[:, 0:nblk], in0=psm[:, 0:nblk], scalar1=mx[:, 15:16], scalar2=None, op0=ALU.is_ge),
                     reads=[('psm',), ('mx', 1)], writes=[('Msel',)])
                mt = Msel[:, nblk - nt_:nblk]
                S.op('dve', lambda e: e.tensor_tensor(out=mt, in0=mt, in1=cblk[:, tb0:32], op=ALU.mult), reads=[('Msel',), ('cblk',)], writes=[('Msel',)])
                S.op('pool', lambda e: e.tensor_copy(out=nme[:, 0:nblk * 64].rearrange("p (j k) -> p j k", k=64),
                                                     in_=Msel[:, 0:nblk].unsqueeze(2).broadcast_to([128, nblk, 64])),
                     reads=[('Msel',)], writes=[('nme',)])
                lastv = nme[:, nblk * 64 - 1024:nblk * 64]
                S.op('pool', lambda e: e.tensor_tensor(out=lastv, in0=lastv, in1=cneg[:, :], op=ALU.mult), reads=[('nme',), ('cneg',)], writes=[('nme',)])
                tiles = []
                for rp in range(r + 1):
                    for cc in range(NCORES):
                        B = 8 * rp + cc
                        tiles.append((ksT[g * 64:(g + 1) * 64, cc, rp * 128:(rp + 1) * 128], ('mul', nme[:, B * 128:(B + 1) * 128]),
                                      vs[:, cc, rp, g, :], [('ksT', cc), ('vs', cc, g), ('nme',)]))
                attn_branch(S, C, A, g, r, tiles, 1, True, 'slc')
                Tl = (n_c - 1) // 128
                tiles = []
                for T in range(Tl + 1):
                    mk = None
                    if r % 2 == 1 and T == Tl:
                        mk = ('add', cmx[:, 64:192])
                    if r % 2 == 0 and T == Tl:
                        mk = ('add', cmx[:, 128:256])
                    if r % 2 == 0 and T == Tl - 1:
                        mk = ('add', cmx[:, 0:128])
                    tiles.append((A.kcT[g * 64:(g + 1) * 64, T * 128:(T + 1) * 128], mk, A.vc[:, T, g, :], [('kcT', T // 4), ('vc', T), ('cmx',)]))
                attn_branch(S, C, A, g, r, tiles, 0, False, 'cmp')
                tiles = []
                for jj in range(jj0, 8):
                    tiles.append((kw[sl][g * 64:(g + 1) * 64, jj + 4, :], ('add', negw[:, (jj + 4) * 128:(jj + 5) * 128]), vw[sl][:, jj + 4, g, :],
                                  wkeys + [('negw',)]))
                attn_branch(S, C, A, g, r, tiles, 2, False, 'win')
                S.op('act', lambda e: e.activation(out=obf[:, :], in_=A.oacc[:].rearrange("p h d -> p (h d)"), func=AF.Copy),
                     reads=[('oacc', 0), ('oacc', 1)], writes=[('obf',)])
                pst = C.PS[6][:].bitcast(BF16)
                for i in range(4):
                    S.op('pe', lambda e: e.transpose(out=pst[:, i * 128:(i + 1) * 128], in_=obf[:, i * 128:(i + 1) * 128], identity=C.ident[:, :]),
                         reads=[('obf',), ('ident',)], writes=[('ps', 6)])
                fk = nfin % 2
                nfin += 1
                S.op('dve', lambda e: e.tensor_copy(out=oT[fk][:].rearrange("p i t -> p (i t)"), in_=pst[:, 0:512]), reads=[('ps', 6)], writes=[('oT', fk)])
                S.dma('sp', Dm['onT'][:, g * 4:(g + 1) * 4, r * 128:(r + 1) * 128], oT[fk][:], reads=[('oT', fk)], writes=[], key=('oTst', fk))
        S.barrier()


def phase_merge(nc, S, C, Dm):
    with ExitStack() as es:
        def sb(name, shape, dtp):
            return es.enter_context(nc.sbuf_tensor("mg_" + name, shape, dtp))
        wno = sb("wno", [128, 8, 1024], BF16)
        wout = sb("wout", [128, 8, 1024], BF16)
        onT = [sb(f"onT{i}", [128, 8, 512], BF16) for i in range(2)]
        sga = [sb(f"sga{i}", [128, 8, 512], BF16) for i in range(2)]
        mb = [sb(f"mb{i}", [128, 8, 512], BF16) for i in range(2)]
        mg = [sb(f"mg{i}", [128, 8, 512], BF16) for i in range(2)]
        tmp = [sb(f"tmp{i}", [128, 512], F32) for i in range(2)]
        x1t = [sb(f"x1t{i}", [128, 1024], F32) for i in range(2)]
        rot = PSRot(range(8))
        for oc in range(8):
            S.dma('pool', wno[:, :, oc * 128:(oc + 1) * 128], Dm['wno'][oc], writes=[('wno', oc)], key=('wno', oc % 2))
        for kc in range(8):
            S.dma('pool', wout[:, kc, :], Dm['wout'][kc * 128:(kc + 1) * 128, :], writes=[('wout', kc)], key=('wout', kc % 2))
        nx = 0
        for ti, (c0, n) in enumerate(TOK_TILES[:4]):
            b = ti % 2
            S.dma('sp', onT[b][:], Dm['onT'][:, :, c0:c0 + n], writes=[('onT', b)], key=('onTl', b))
            S.dma('sp', sga[b][:], Dm['sga'][:, :, c0:c0 + n], writes=[('sga', b)], key=('sgal', b))
            S.dma('sp', mb[b][:], Dm['mb'][:, :, c0:c0 + n], writes=[('mb', b)], key=('mbl', b))
            for oc in range(8):
                pi = rot.next()
                for kc in range(8):
                    mm(S, C.PS[pi][:, :], wno[:, kc, oc * 128:(oc + 1) * 128], onT[b][:, kc, :], kc == 0, kc == 7,
                       reads=[('wno', oc), ('onT', b)], writes=[('ps', pi)])
                tk = oc % 2
                S.op('dve', lambda e: e.tensor_tensor(out=tmp[tk][:], in0=sga[b][:, oc, :], in1=C.PS[pi][:, :], op=ALU.mult),
                     reads=[('ps', pi), ('sga', b)], writes=[('tmp', tk)])
                S.op('pool', lambda e: e.tensor_tensor(out=mg[b][:, oc, :], in0=tmp[tk][:], in1=mb[b][:, oc, :], op=ALU.add),
                     reads=[('tmp', tk), ('mb', b)], writes=[('mg', b, oc)])
            for u4 in range(4):
                u = ti * 4 + u4
                xk = nx % 2
                nx += 1
                S.dma('sp', x1t[xk][:], Dm['x1'][u * 128:(u + 1) * 128, :], writes=[('x1t', xk)], key=('x1l', xk))
                for nh in range(2):
                    pi = rot.next()
                    for oc in range(8):
                        mm(S, C.PS[pi][:, :], mg[b][:, oc, u4 * 128:(u4 + 1) * 128], wout[:, oc, nh * 512:(nh + 1) * 512], oc == 0, oc == 7,
                           reads=[('wout', oc), ('mg', b, oc)], writes=[('ps', pi)])
                    S.op('dve', lambda e: e.tensor_tensor(out=x1t[xk][:, nh * 512:(nh + 1) * 512], in0=x1t[xk][:, nh * 512:(nh + 1) * 512],
                                                          in1=C.PS[pi][:, :], op=ALU.add),
                         reads=[('ps', pi), ('x1t', xk)], writes=[('x1t', xk)])
                S.dma('sp', Dm['x2'][u * 128:(u + 1) * 128, :], x1t[xk][:], reads=[('x1t', xk)], writes=[], key=('x2st', xk))
        S.barrier()


def build_program(stop_after=None, debug=()):
    nc = bass.Bass("TRN2", target_bir_lowering=False)
    dt = nc.dram_tensor
    Dm = {'_debug': debug}

    def din(name, shape, dtype=F32):
        Dm[name] = dt(name, list(shape), dtype, kind="ExternalInput").ap()

    def dscr(name, shape, dtype):
        if name in debug:
            t = dt("dbg_" + name, list(shape), dtype, kind="ExternalOutput")
        else:
            t = dt("scr_" + name, list(shape), dtype)
        Dm[name + "_t"] = t
        Dm[name] = t.ap()

    din("x", [NTL + NHALO, D])
    din("ident", [128, 128])
    din("g1", [128, 8])
    din("gm", [128, 8])
    din("wg1", [NFC, 128, 8, 128])
    din("wu1", [NFC, 128, 8, 128])
    din("wd1", [DFF, D])
    din("win", [NWCH, 128, 8, 128])
    din("convw", [128, 8, 3])
    din("wco", [8, 128, 8, 128])
    dscr("x1", [NTL + NHALO, D], F32)
    if 'hT_in' in debug:
        din("hT", [128, 17, 8, 128], BF16)
    else:
        dscr("hT", [128, 17, 8, 128], BF16)
    dscr("qT", [128, 8, NTL], BF16)
    dscr("mb", [128, 8, NTL], BF16)
    dscr("sga", [128, 8, NTL], BF16)
    din("w1", [2, 128, 16, 256])
    din("pecol", [2, 128, 16])
    din("w2kpad", [2, 128, 2, 128])
    din("w2v", [128, 2, 64])
    din("cmaskx", [128, 256])
    din("causalneg", [128, 1024])
    din("negw", [128, 1536])
    din("tblm", [128, 32])
    din("tbla", [128, 32])
    din("cblk", [128, 32])
    din("wno", [8, 128, 8, 128])
    din("wout", [D, D])
    din("g2", [128, 8])
    din("wg2", [NFC, 128, 8, 128])
    din("wu2", [NFC, 128, 8, 128])
    din("wd2", [DFF, D])
    din("gfin", [128, D])
    dscr("onT", [128, 8, NTL], BF16)
    dscr("x2", [NTL, D], F32)
    dscr("kcT", [128, 1024], BF16)
    dscr("vc", [128, 8, 2, 65], BF16)
    Dm['y'] = dt("y", [NTL, D], F32, kind="ExternalOutput").ap()
    Dm['xb_in_t'] = dt("scr_xb_in", [768, NTL], BF16)
    Dm['xb_in'] = Dm['xb_in_t'].ap()
    Dm['xg_t'] = dt("scr_xg", [NCORES * 768, NTL], BF16)
    Dm['xg'] = Dm['xg_t'].ap()
    if 'xg' in debug:
        Dm['dbg_xg'] = dt("dbg_xg", [NCORES * 768, NTL], BF16, kind="ExternalOutput").ap()
    if 'g3' in debug:
        Dm['dbg_g3'] = dt("dbg_g3", [128, NLB, 48], F32, kind="ExternalOutput").ap()

    with ExitStack() as es:
        block = es.enter_context(nc.Block())
        S = Sched(nc, es)
        C = Ctx()

        def sbg(name, shape, dtp):
            return es.enter_context(nc.sbuf_tensor("sb_" + name, shape, dtp))
        C.PS = [es.enter_context(nc.psum_tensor(f"ps{i}", [128, 512], F32)) for i in range(8)]
        C.ident = sbg("ident", [128, 128], BF16)
        C.junk = sbg("junk", [128, 1024], BF16)
        C.eps_t = sbg("eps_t", [128, 16], F32)
        C.ssq = sbg("ssq", [128, 16 * 16], F32)
        C.sq = sbg("sq", [128, 16 * 16], F32)
        C.rstd = sbg("rstd", [128, 16 * 16], F32)
        C.g3 = sbg("g3", [128, NLB, 48], F32)

        @block.gpsimd
        def _(_g):
            S.dma('pool', C.ident[:], Dm['ident'][:, :], writes=[('ident',)], key=('ident',))
            S.op('dve', lambda e: e.memset(C.eps_t[:], EPS), writes=[('eps',)])
            units = [(r * 128, 128) for r in range(NLB)] + [(NTL, NHALO)]

            def finish():
                if 'xg' in debug:
                    for i in range(24):
                        S.dma('sp', Dm['dbg_xg'][i * 256:(i + 1) * 256, :], Dm['xg'][i * 256:(i + 1) * 256, :], reads=[('xg',)], writes=[], key=('dbgxg',))
                if 'g3' in debug:
                    S.dma('sp', Dm['dbg_g3'][:, :, :], C.g3[:], reads=[('g3', u) for u in range(NLB)], writes=[], key=('dbgg3',))
                S.final_wait('sp')

            def epi1(S_, C_, P, ui, unit, pys, xs, init=False, sb=None, state=None):
                if init:
                    st = Ctx()
                    st.x1t = [sb(f"x1t{i}", [128, 1024], F32) for i in range(2)]
                    st.hn = [sb(f"hn{i}", [128, 1024], BF16) for i in range(2)]
                    st.hT = [sb(f"hTt{i}", [128, 8, 128], BF16) for i in range(2)]
                    st.gm = sb("gm", [128, 8], F32)
                    st.gmbc = sb("gmbc", [128, 8, 128], F32)
                    st.n = 0
                    st.first = True
                    return st
                st = state
                if st.first:
                    st.first = False
                    S.dma('sp', st.gm[:], Dm['gm'][:, :], writes=[('gm',)], key=('gm',))
                    S.op('dve', lambda e: e.tensor_copy(out=st.gmbc[:], in_=st.gm[:].unsqueeze(2).broadcast_to([128, 8, 128])),
                         reads=[('gm',)], writes=[('gmbc',)])
                r0, nb = unit
                k = st.n % 2
                st.n += 1
                xs_ap, xs_key = xs
                for nh in range(2):
                    S.op('dve', lambda e, nh=nh: e.scalar_tensor_tensor(out=st.x1t[k][0:nb, nh * 512:(nh + 1) * 512],
                                                                        in0=C.PS[pys[nh]][0:nb, :], scalar=0.5,
                                                                        in1=xs_ap[:, nh * 512:(nh + 1) * 512], op0=ALU.mult, op1=ALU.add),
                         reads=[('ps', pys[nh]), xs_key], writes=[('x1t', k, nh)])
                x1k = [('x1t', k, 0), ('x1t', k, 1)]
                S.dma('sp', Dm['x1'][r0:r0 + nb, :], st.x1t[k][0:nb, :], reads=x1k, writes=[], key=('x1st', k))
                rms_prologue(S, C, st.x1t[k][0:nb, :], nb, 10 + k, st.hn[k], ('hn', k), x1k, ('epi1', k))
                transpose_unit(S, C, st.hn[k], ('hn', k), nb, 6 + k, st.hT[k][:, :, 0:nb], ('hTt', k), st.gmbc)
                S.dma('sp', Dm['hT'][:, ui, :, 0:nb], st.hT[k][:, :, 0:nb], reads=[('hTt', k)], writes=[], key=('hTst', k))

            if 'hT_in' not in debug:
                ffn_phase(nc, S, C, "f1", Dm['x'], units, Dm['g1'], Dm['wg1'], Dm['wu1'], Dm['wd1'], epi1)
            if stop_after == 'ffn1':
                return finish()
            phase_proj(nc, S, C, Dm)
            if stop_after == 'proj':
                return finish()
            phase_conv(nc, S, C, Dm)
            if stop_after == 'conv':
                return finish()
            with ExitStack() as es_att:
                A = Ctx()
                A.kcT = es_att.enter_context(nc.sbuf_tensor("A_kcT", [128, 1024], BF16))
                A.vc = es_att.enter_context(nc.sbuf_tensor("A_vc", [128, 8, 2, 65], BF16))
                phase_cmp(nc, S, C, Dm, A)
                if 'kcT' in debug:
                    S.dma('sp', Dm['kcT'][:, :], A.kcT[:], key=('dbgk', 0))
                    S.dma('sp', Dm['vc'][:, :, :, :], A.vc[:], key=('dbgk', 1))
                if stop_after == 'cmp':
                    return finish()
                phase_attn(nc, S, C, Dm, A)
            if stop_after == 'attn':
                return finish()
            phase_merge(nc, S, C, Dm)
            if stop_after == 'merge':
                return finish()

            def epi2(S_, C_, P, ui, unit, pys, xs, init=False, sb=None, state=None):
                if init:
                    st = Ctx()
                    st.x3 = [sb(f"x3t{i}", [128, 1024], F32) for i in range(2)]
                    st.yo = [sb(f"yo{i}", [128, 1024], F32) for i in range(2)]
                    st.gf = sb("gfin", [128, 1024], F32)
                    st.n = 0
                    st.first = True
                    return st
                st = state
                if st.first:
                    st.first = False
                    S.dma('sp', st.gf[:], Dm['gfin'][:, :], writes=[('gfin',)], key=('gfin',))
                r0, nb = unit
                k = st.n % 2
                st.n += 1
                xs_ap, xs_key = xs
                for nh in range(2):
                    S.op('dve', lambda e, nh=nh: e.scalar_tensor_tensor(out=st.x3[k][0:nb, nh * 512:(nh + 1) * 512],
                                                                        in0=C.PS[pys[nh]][0:nb, :], scalar=0.5,
                                                                        in1=xs_ap[:, nh * 512:(nh + 1) * 512], op0=ALU.mult, op1=ALU.add),
                         reads=[('ps', pys[nh]), xs_key], writes=[('x3t', k, nh)])
                x3k = [('x3t', k, 0), ('x3t', k, 1)]
                slot = 10 + k
                ssq_col = C.ssq[0:nb, slot * 16:slot * 16 + 1]
                sq_col = C.sq[0:nb, slot * 16:slot * 16 + 1]
                rstd_col = C.rstd[0:nb, slot * 16:slot * 16 + 1]
                S.op('act', lambda e: e.activation(out=C.junk[0:nb, :], in_=st.x3[k][0:nb, :], func=AF.Square, accum_out=ssq_col),
                     reads=x3k, writes=[('junk',), ('e2ssq', k)])
                S.op('act', lambda e: e.activation(out=sq_col, in_=ssq_col, func=AF.Sqrt, bias=C.eps_t[0:nb, 0:1], scale=1.0 / D),
                     reads=[('e2ssq', k), ('e2rstd', k)], writes=[('e2sq', k)])
                S.op('dve', lambda e: e.reciprocal(out=rstd_col, in_=sq_col), reads=[('e2sq', k)], writes=[('e2rstd', k)])
                S.op('dve', lambda e: e.scalar_tensor_tensor(out=st.yo[k][0:nb, :], in0=st.x3[k][0:nb, :], scalar=rstd_col, in1=st.gf[0:nb, :],
                                                             op0=ALU.mult, op1=ALU.mult),
                     reads=x3k + [('e2rstd', k), ('gfin',)], writes=[('yo', k)])
                S.dma('sp', Dm['y'][r0:r0 + nb, :], st.yo[k][0:nb, :], reads=[('yo', k)], writes=[], key=('yst', k))

            ffn_phase(nc, S, C, "f2", Dm['x2'], units[:NLB], Dm['g2'], Dm['wg2'], Dm['wu2'], Dm['wd2'], epi2)
            finish()
    print("sems", S.nsem, "waits", S.nwait, "counts", S.cnt)
    return nc


def relayout_kcf(W):
    K, n = W.shape
    return np.ascontiguousarray(W.reshape(K // 128, 128, n // 128, 128).transpose(2, 1, 0, 3))


def vec_pk(g):
    return np.ascontiguousarray(g.reshape(8, 128).T)


def core_tokens(c):
    r = np.arange(NLB)[:, None]
    p = np.arange(128)[None, :]
    return (128 * (8 * r + c) + p).reshape(-1)


def core_consts(c):
    p = np.arange(128)[:, None]
    trel = 128 * c + p
    crel = np.arange(-128, 128)[None, :]
    cmaskx = np.where(16 * crel + 31 <= trel, 0.0, NEG).astype(np.float32)
    jrel = np.arange(-16, 16)[None, :]
    cur = 2 * c + p // 64
    forced = (jrel == cur) | (jrel == cur - 1)
    causal = jrel <= cur
    tblm = np.where(forced, 0.0, np.where(causal, 1.0, 0.0)).astype(np.float32)
    tbla = np.where(forced, 1e9, np.where(causal, 0.0, -1.0)).astype(np.float32)
    cblk = causal.astype(np.float32)
    kpos = np.arange(1024)[None, :]
    causalneg = np.where(kpos <= trel, 1.0, 0.0).astype(np.float32)
    wpos = np.arange(-512, 1024)[None, :]
    diff = trel - wpos
    negw = np.where((diff >= 0) & (diff < 512), 0.0, NEG).astype(np.float32)
    return dict(cmaskx=cmaskx, tblm=tblm, tbla=tbla, cblk=cblk, causalneg=causalneg, negw=negw)


def prep_inputs(inp):
    f = lambda k: np.asarray(inp[k], np.float32)
    x = f("x")[0]
    w_in = f("w_in")[0]
    q = w_in[:, 0:1024]
    qperm = np.concatenate([np.concatenate([q[:, i * 64:(i + 1) * 64], q[:, (8 + i) * 64:(9 + i) * 64]], axis=1) for i in range(8)], axis=1)
    sec = lambda a, b: w_in[:, a:b]
    nsag = np.zeros((1024, 128), np.float32)
    nsag[:, :48] = sec(1792, 1840)
    wcat = np.concatenate([qperm, sec(1024, 1152), sec(1152, 1280), sec(1280, 1408), sec(1536, 1664),
                           sec(1408, 1536), sec(1664, 1792), nsag,
                           sec(1840, 2864), sec(2864, 3888), sec(3888, 4912), sec(4912, 5936), sec(5936, 6960)], axis=1)
    assert wcat.shape[1] == NWCH * 128
    common = {
        "ident": np.eye(128, dtype=np.float32),
        "g1": vec_pk(f("ffn1_norm")[0]),
        "gm": vec_pk(f("mix_norm")[0]),
        "wg1": relayout_kcf(f("ffn1_w_gate")[0]),
        "wu1": relayout_kcf(f("ffn1_w_up")[0]),
        "wd1": np.ascontiguousarray(f("ffn1_w_down")[0]),
        "win": relayout_kcf(wcat),
        "convw": np.ascontiguousarray(f("conv_w")[0].reshape(3, 8, 128).transpose(2, 1, 0)),
        "wco": relayout_kcf(f("w_conv_out")[0]),
        "w1": np.ascontiguousarray(np.stack([f("cmp_k_w1")[0], f("cmp_v_w1")[0]]).reshape(2, 16, 128, 256).transpose(0, 2, 1, 3)),
        "pecol": np.ascontiguousarray(np.stack([f("cmp_pe_k")[0], f("cmp_pe_v")[0]]).reshape(2, 16, 128).transpose(0, 2, 1)),
        "w2v": np.ascontiguousarray(f("cmp_v_w2")[0].reshape(2, 128, 64).transpose(1, 0, 2)),
        "wno": relayout_kcf(f("w_nsa_out")[0]),
        "wout": np.ascontiguousarray(f("w_out")[0]),
        "g2": vec_pk(f("ffn2_norm")[0]),
        "wg2": relayout_kcf(f("ffn2_w_gate")[0]),
        "wu2": relayout_kcf(f("ffn2_w_up")[0]),
        "wd2": np.ascontiguousarray(f("ffn2_w_down")[0]),
        "gfin": np.ascontiguousarray(np.broadcast_to(f("final_norm")[None, :], (128, D))),
    }
    w2k = f("cmp_k_w2")[0].reshape(2, 128, 64)
    w2kpad = np.zeros((2, 128, 2, 128), np.float32)
    for g in range(2):
        w2kpad[g, :, :, g * 64:(g + 1) * 64] = w2k.transpose(1, 0, 2)
    common["w2kpad"] = w2kpad
    maps = []
    for c in range(NCORES):
        tok = core_tokens(c)
        xc = np.zeros((NTL + NHALO, D), np.float32)
        xc[:NTL] = x[tok]
        for r in range(NLB):
            B = 8 * r + c
            if B > 0:
                xc[NTL + 2 * r: NTL + 2 * r + 2] = x[128 * B - 2:128 * B]
        m = dict(common)
        m["x"] = xc
        m.update(core_consts(c))
        maps.append(m)
    return maps


def kernel(**inputs):
    nc = build_program()
    maps = prep_inputs(inputs)
    res = run_bass_kernel_spmd(nc, maps, core_ids=list(range(NCORES)))
    out = np.zeros((1, SEQ, D), np.float32)
    for c in range(NCORES):
        out[0, core_tokens(c)] = res.results[c]["y"]
    return out
```

```python
import numpy as np
from contextlib import ExitStack
import concourse.bass as bass
import concourse.mybir as mybir
from concourse.bass_utils import run_bass_kernel_spmd

F32 = mybir.dt.float32
BF16 = mybir.dt.bfloat16
AF = mybir.ActivationFunctionType
ALU = mybir.AluOpType

NCORES = 8
D = 1024
DFF = 2816
NFC = DFF // 128
SEQ = 16384
NLB = 16
NTL = NLB * 128
NHALO = 32
EPS = 1e-6
NEG = -30000.0


class Sched:
    EPOCH = 12000

    def __init__(self, nc, es):
        self.nc = nc
        self.es = es
        self.eng = dict(pe=nc.tensor, act=nc.scalar, dve=nc.vector, pool=nc.gpsimd, sp=nc.sync)
        self.cnt = {e: 0 for e in self.eng}
        self.esem = {e: [] for e in self.eng}
        self.seen = {e: {} for e in self.eng}
        self.res = {}
        self.dsem = {}
        self.nsem = 0
        self.nwait = 0
        self.free = []
        self.snap = {e: [] for e in self.eng}

    def _newsem(self, name):
        self.nsem += 1
        return self.es.enter_context(self.nc.semaphore(name))

    def _esem(self, e, epoch):
        while len(self.esem[e]) <= epoch:
            self.esem[e].append(self._newsem(f"e_{e}_{len(self.esem[e])}"))
        return self.esem[e][epoch]

    def _wait(self, e, tok, same_ok=True):
        if tok[0] == 'eng':
            _, f, seq = tok
            if f == e and same_ok and e == 'pe':
                return
            if self.seen[e].get(f, 0) >= seq:
                return
            self.seen[e][f] = seq
            ep = (seq - 1) // self.EPOCH
            self.eng[e].wait_ge(self._esem(f, ep), seq - ep * self.EPOCH)
            self.nwait += 1
            if f != e and seq - 1 < len(self.snap[f]):
                for k, v in self.snap[f][seq - 1].items():
                    if k != e and self.seen[e].get(k, 0) < v:
                        self.seen[e][k] = v
        else:
            _, key, val = tok
            sem, cur = self.dsem[key]
            if self.seen[e].get(('d', key), 0) >= val:
                return
            self.seen[e][('d', key)] = cur
            self.eng[e].wait_ge(sem, cur)
            self.nwait += 1

    def _deps(self, reads, writes):
        deps = []
        for r in reads:
            R = self.res.get(r)
            if R and R['w']:
                deps.append(R['w'])
            if R and r[0] == 'ps':
                deps.extend(R['r'].values())
        for w in writes:
            R = self.res.get(w)
            if R:
                if R['w']:
                    deps.append(R['w'])
                deps.extend(R['r'].values())
        return deps

    def _record(self, tok, reads, writes):
        for r in reads:
            R = self.res.setdefault(r, {'w': None, 'r': {}})
            R['r'][(tok[0], tok[1])] = tok
        for w in writes:
            self.res[w] = {'w': tok, 'r': {}}

    def op(self, e, fn, reads=(), writes=()):
        for d in self._deps(reads, writes):
            self._wait(e, d)
        inst = fn(self.eng[e])
        self.cnt[e] += 1
        self.snap[e].append(dict(self.seen[e]))
        seq = self.cnt[e]
        ep = (seq - 1) // self.EPOCH
        inst.then_inc(self._esem(e, ep), 1)
        self._record(('eng', e, seq), reads, writes)

    def dma(self, q, out, in_, reads=(), writes=(), key=None, **kw):
        for d in self._deps(reads, writes):
            self._wait(q, d, same_ok=False)
        if key not in self.dsem:
            if self.free:
                self.dsem[key] = self.free.pop()
            else:
                self.dsem[key] = [self._newsem("d_%d" % self.nsem), 0]
        ent = self.dsem[key]
        self.eng[q].dma_start(out=out, in_=in_, **kw).then_inc(ent[0], 16)
        ent[1] += 16
        self._record(('dma', key, ent[1]), reads, writes)

    def raw(self, e, fn, reads=(), writes=(), inc=1):
        for d in self._deps(reads, writes):
            self._wait(e, d, same_ok=False)
        key = ('raw', self.nsem)
        sem = self._newsem("r_%d" % self.nsem)
        fn(self.eng[e]).then_inc(sem, inc) if inc != 1 else fn(self.eng[e]).then_inc(sem)
        self.dsem[key] = [sem, inc]
        self._record(('dma', key, inc), reads, writes)

    def barrier(self):
        for e in self.eng:
            for f in self.eng:
                if f != e and self.cnt[f] > 0:
                    self._wait(e, ('eng', f, self.cnt[f]))
            for key, (sem, cur) in self.dsem.items():
                if cur > 0:
                    self._wait(e, ('dma', key, cur))
        self.res = {}
        for key in list(self.dsem.keys()):
            if key[0] != 'raw':
                self.free.append(self.dsem.pop(key))

    def final_wait(self, e='sp'):
        for f in self.eng:
            if f != e and self.cnt[f] > 0:
                self._wait(e, ('eng', f, self.cnt[f]))
        for key, (sem, cur) in self.dsem.items():
            if cur > 0:
                self._wait(e, ('dma', key, cur))


def mm(S, out, lhsT, rhs, start, stop, reads, writes, **kw):
    S.op('pe', lambda e: e.matmul(out, lhsT, rhs, start=start, stop=stop, **kw), reads=reads, writes=writes)


class Ctx:
    pass


def rms_prologue(S, C, src_ap, nb, slot, xn_tile, xn_key, src_reads, tag):
    ssq_col = C.ssq[0:nb, slot * 16:slot * 16 + 1]
    sq_col = C.sq[0:nb, slot * 16:slot * 16 + 1]
    rstd_col = C.rstd[0:nb, slot * 16:slot * 16 + 1]
    S.op('act', lambda e: e.activation(out=C.junk[0:nb, :], in_=src_ap, func=AF.Square, accum_out=ssq_col),
         reads=src_reads, writes=[('junk',), (tag, 'ssq')])
    S.op('act', lambda e: e.activation(out=sq_col, in_=ssq_col, func=AF.Sqrt, bias=C.eps_t[0:nb, 0:1], scale=1.0 / D),
         reads=[(tag, 'ssq'), (tag, 'rstd')], writes=[(tag, 'rstd0')])
    S.op('dve', lambda e: e.reciprocal(out=rstd_col, in_=sq_col), reads=[(tag, 'rstd0')], writes=[(tag, 'rstd')])
    S.op('dve', lambda e: e.tensor_scalar(out=xn_tile[0:nb, :], in0=src_ap, scalar1=rstd_col, scalar2=None, op0=ALU.mult),
         reads=list(src_reads) + [(tag, 'rstd')], writes=[xn_key])


def transpose_unit(S, C, xn_tile, xn_key, nb, ps_idx, dst_ap, dst_key, gbc):
    pst = C.PS[ps_idx][:].bitcast(BF16)
    for kc in range(8):
        S.op('pe', lambda e, kc=kc: e.transpose(out=pst[:, kc * 128:kc * 128 + nb], in_=xn_tile[0:nb, kc * 128:(kc + 1) * 128],
                                               identity=C.ident[0:nb, 0:nb]),
             reads=[xn_key], writes=[('ps', ps_idx)])
    src = pst.rearrange("p (k t) -> p k t", k=8)[:, :, 0:nb]
    S.op('dve', lambda e: e.tensor_tensor(out=dst_ap, in0=src, in1=gbc[:, :, 0:nb], op=ALU.mult),
         reads=[('ps', ps_idx)], writes=[dst_key])


def ffn_phase(nc, S, C, tag, x_d, units, g_d, wg_d, wu_d, wd_d, epilogue):
    with ExitStack() as es:
        def sb(name, shape, dt):
            return es.enter_context(nc.sbuf_tensor(f"{tag}_{name}", shape, dt))
        P = Ctx()
        P.wd = sb("wd", [128, NFC, 1024], BF16)
        P.xs = sb("xs", [128, 9, 1024], F32)
        P.xnT = sb("xnT", [128, 8, 1056], BF16)
        P.H = sb("H", [128, NFC, 1056], BF16)
        P.wg = [sb(f"wg{i}", [128, 8, 128], BF16) for i in range(3)]
        P.wu = [sb(f"wu{i}", [128, 8, 128], BF16) for i in range(3)]
        P.xn = [sb(f"xn{i}", [128, 1024], BF16) for i in range(2)]
        P.sg = [sb(f"sg{i}", [128, 512], BF16) for i in range(2)]
        P.g = sb("g", [128, 8], F32)
        P.gbc = sb("gbc", [128, 8, 128], F32)
        P.ep = es
        epi_state = epilogue(None, None, P, None, None, None, None, init=True, sb=sb)

        S.dma('sp', P.g[:], g_d[:, :], writes=[(tag, 'g')], key=(tag, 'g'))
        S.op('dve', lambda e: e.tensor_copy(out=P.gbc[:], in_=P.g[:].unsqueeze(2).broadcast_to([128, 8, 128])),
             reads=[(tag, 'g')], writes=[(tag, 'gbc')])
        for fc in range(NFC):
            S.dma('pool', P.wd[:, fc, :], wd_d[fc * 128:(fc + 1) * 128, :], writes=[(tag, 'wd', fc)], key=(tag, 'wd', fc % 2))

        npass = 2
        per = len([u for u in units if u[1] == 128]) // npass
        wcount = 0
        for ps in range(npass):
            pun = [(i, u) for i, u in enumerate(units) if u[1] == 128][ps * per:(ps + 1) * per]
            if ps == npass - 1:
                pun += [(i, u) for i, u in enumerate(units) if u[1] != 128]
            cols = []
            c0 = 0
            for (ui, (r0, nb)) in pun:
                cols.append(c0)
                c0 += nb
            ncols = c0
            for si, (ui, (r0, nb)) in enumerate(pun):
                S.dma('sp', P.xs[0:nb, si, :], x_d[r0:r0 + nb, :], writes=[(tag, 'xs', si)], key=(tag, 'xs', si))
                xnk = (tag, 'xn', si % 2)
                rms_prologue(S, C, P.xs[0:nb, si, :], nb, si, P.xn[si % 2], xnk,
                             [(tag, 'xs', si)], (tag, 'pro', si))
                transpose_unit(S, C, P.xn[si % 2], xnk, nb, 6 + si % 2, P.xnT[:, :, cols[si]:cols[si] + nb], (tag, 'xnT', si), P.gbc)
            subt = []
            c = 0
            while c < ncols:
                n = min(512, ncols - c) if (ncols - c) >= 512 else ncols - c
                subt.append((c, n))
                c += n
            xnT_keys = [(tag, 'xnT', si) for si in range(len(pun))]
            ev = 0
            for fc in range(NFC):
                wb = wcount % 3
                wcount += 1
                S.dma('pool', P.wg[wb][:], wg_d[fc], reads=[], writes=[(tag, 'wg', wb)], key=(tag, 'wg', wb))
                S.dma('pool', P.wu[wb][:], wu_d[fc], reads=[], writes=[(tag, 'wu', wb)], key=(tag, 'wu', wb))
                for (c, n) in subt:
                    pb = ev % 2
                    ev += 1
                    pg = C.PS[0 + pb]
                    pu = C.PS[2 + pb]
                    for kc in range(8):
                        mm(S, pg[:, 0:n], P.wg[wb][:, kc, :], P.xnT[:, kc, c:c + n], kc == 0, kc == 7,
                           reads=[(tag, 'wg', wb)] + xnT_keys, writes=[('ps', 0 + pb)])
                    for kc in range(8):
                        mm(S, pu[:, 0:n], P.wu[wb][:, kc, :], P.xnT[:, kc, c:c + n], kc == 0, kc == 7,
                           reads=[(tag, 'wu', wb)] + xnT_keys, writes=[('ps', 2 + pb)])
                    S.op('act', lambda e: e.activation(out=P.sg[pb][:, 0:n], in_=pg[:, 0:n], func=AF.Silu),
                         reads=[('ps', 0 + pb)], writes=[(tag, 'sg', pb)])
                    S.op('dve', lambda e: e.tensor_tensor(out=P.H[:, fc, c:c + n], in0=P.sg[pb][:, 0:n], in1=pu[:, 0:n], op=ALU.mult),
                         reads=[(tag, 'sg', pb), ('ps', 2 + pb)], writes=[(tag, 'H', fc, c)])
            for si, (ui, (r0, nb)) in enumerate(pun):
                c = cols[si]
                hkeys = [(tag, 'H', fc, cc) for fc in range(NFC) for (cc, nn) in subt if cc <= c < cc + nn]
                pys = []
                for nh in range(2):
                    pi = 4 + nh
                    py = C.PS[pi]
                    for fc in range(NFC):
                        mm(S, py[0:nb, :], P.H[:, fc, c:c + nb], P.wd[:, fc, nh * 512:(nh + 1) * 512], fc == 0, fc == NFC - 1,
                           reads=[(tag, 'wd', fc), (tag, 'H', fc, [cc for (cc, nn) in subt if cc <= c < cc + nn][0])],
                           writes=[('ps', pi)])
                    pys.append(pi)
                epilogue(S, C, P, ui, (r0, nb), pys, (P.xs[0:nb, si, :], (tag, 'xs', si)), init=False, sb=None, state=epi_state)
        S.barrier()


class PSRot:
    def __init__(self, idxs):
        self.idxs = list(idxs)
        self.n = 0

    def next(self):
        i = self.idxs[self.n % len(self.idxs)]
        self.n += 1
        return i


TOK_TILES = [(0, 512), (512, 512), (1024, 512), (1536, 512), (2048, NHALO)]
CH_Q, CH_FM, CH_TM, CH_B, CH_C, CH_X, CH_GA, CH_GB = 0, 8, 12, 15, 23, 31, 39, 47
NWCH = 55


_uid = [0]


def load_hT(nc, S, es, Dm):
    _uid[0] += 1
    hT = es.enter_context(nc.sbuf_tensor("hT_all%d" % _uid[0], [128, 8, NTL + NHALO], BF16))
    for u in range(17):
        nb = 128 if u < 16 else NHALO
        S.dma('sp', hT[:, :, u * 128:u * 128 + nb], Dm['hT'][:, u, :, 0:nb], writes=[('hT', u)], key=('hTl', u % 4))
    return hT


def hT_keys(c0, n):
    return [('hT', u) for u in range(c0 // 128, (c0 + n - 1) // 128 + 1)]


def proj_fm(S, C, hT, w, wkey, c0, n, pi):
    for kc in range(8):
        mm(S, C.PS[pi][:, 0:n], w[:, kc, :], hT[:, kc, c0:c0 + n], kc == 0, kc == 7,
           reads=[wkey] + hT_keys(c0, n), writes=[('ps', pi)])


def phase_proj(nc, S, C, Dm):
    with ExitStack() as es:
        def sb(name, shape, dtp):
            return es.enter_context(nc.sbuf_tensor("pj_" + name, shape, dtp))
        hT = load_hT(nc, S, es, Dm)
        qT = sb("qT", [128, 8, NTL], BF16)
        fm = sb("fm", [128, 4, NTL], BF16)
        tm = sb("tm", [128, NLB, 256], BF16)
        wtm = sb("wtm", [128, 8, 384], BF16)
        wb = [sb(f"w{i}", [128, 8, 128], BF16) for i in range(3)]
        rot = PSRot(range(8))
        win = Dm['win']
        for j in range(3):
            S.dma('pool', wtm[:, :, j * 128:(j + 1) * 128], win[CH_TM + j], writes=[('wtm',)], key=('wtm',))
        nw = 0
        ev = 0
        for ch in list(range(CH_FM, CH_FM + 4)):
            k = nw % 3
            nw += 1
            S.dma('pool', wb[k][:], win[ch], writes=[('pjw', k)], key=('pjw', k))
            for (c0, n) in TOK_TILES[:4]:
                pi = rot.next()
                proj_fm(S, C, hT, wb[k], ('pjw', k), c0, n, pi)
                dst = fm[:, ch - CH_FM, c0:c0 + n]
                eng = 'act' if ev % 2 == 0 else 'dve'
                ev += 1
                if eng == 'act':
                    S.op('act', lambda e: e.activation(out=dst, in_=C.PS[pi][:, 0:n], func=AF.Copy), reads=[('ps', pi)], writes=[('fm', ch, c0)])
                else:
                    S.op('dve', lambda e: e.tensor_copy(out=dst, in_=C.PS[pi][:, 0:n]), reads=[('ps', pi)], writes=[('fm', ch, c0)])
        fmkeys = [('fm', ch, c0) for ch in range(CH_FM, CH_FM + 4) for (c0, n) in TOK_TILES[:4]]
        S.dma('sp', Dm['xb_in'][0:512, :].rearrange("(k p) t -> p k t", p=128), fm[:], reads=fmkeys, writes=[('xb_in', 'fm')], key=('xbst', 0))
        if 'pj1' in Dm['_debug']:
            S.barrier()
            return
        for u in range(NLB):
            pi = rot.next()
            for kc in range(8):
                mm(S, C.PS[pi][:, 0:304], hT[:, kc, u * 128:(u + 1) * 128], wtm[:, kc, 0:304], kc == 0, kc == 7,
                   reads=[('wtm',), ('hT', u)], writes=[('ps', pi)])
            S.op('act', lambda e: e.activation(out=tm[:, u, :], in_=C.PS[pi][:, 0:256], func=AF.Copy), reads=[('ps', pi)], writes=[('tm', u)])
            S.op('act', lambda e: e.activation(out=C.g3[:, u, :], in_=C.PS[pi][:, 256:304], func=AF.Sigmoid), reads=[('ps', pi)], writes=[('g3', u)])
        tmview = Dm['xb_in'][512:768, :].rearrange("a (b c) -> (a b) c", c=256).rearrange("(u p) c -> p u c", p=128)
        for u4 in range(4):
            S.dma('sp', tmview[:, u4 * 4:(u4 + 1) * 4, :], tm[:, u4 * 4:(u4 + 1) * 4, :], reads=[('tm', u) for u in range(u4 * 4, u4 * 4 + 4)],
                  writes=[('xb_in', 'tm', u4)], key=('xbst', 1))
        if 'pj2' in Dm['_debug']:
            S.barrier()
            return
        if 'no_cc' not in Dm['_debug']:
          S.raw('pool', lambda e: e.collective_compute("AllGather", ALU.bypass, replica_groups=[list(range(NCORES))],
                                                    ins=[Dm['xb_in_t'].ap().opt()], outs=[Dm['xg_t'].ap().opt()]),
                reads=[('xb_in', 'fm')] + [('xb_in', 'tm', u4) for u4 in range(4)], writes=[('xg',)])
        for ch in range(CH_Q, CH_Q + 8):
            k = nw % 3
            nw += 1
            S.dma('pool', wb[k][:], win[ch], writes=[('pjw', k)], key=('pjw', k))
            for (c0, n) in TOK_TILES[:4]:
                pi = rot.next()
                proj_fm(S, C, hT, wb[k], ('pjw', k), c0, n, pi)
                dst = qT[:, ch, c0:c0 + n]
                eng = 'act' if ev % 2 == 0 else 'dve'
                ev += 1
                if eng == 'act':
                    S.op('act', lambda e: e.activation(out=dst, in_=C.PS[pi][:, 0:n], func=AF.Copy), reads=[('ps', pi)], writes=[('qT', ch, c0)])
                else:
                    S.op('dve', lambda e: e.tensor_copy(out=dst, in_=C.PS[pi][:, 0:n]), reads=[('ps', pi)], writes=[('qT', ch, c0)])
        S.dma('sp', Dm['qT'][:, :, :], qT[:], reads=[('qT', ch, c0) for ch in range(8) for (c0, n) in TOK_TILES[:4]], writes=[('qT_d',)], key=('qTst',))
        S.barrier()


def phase_conv(nc, S, C, Dm):
    with ExitStack() as es:
        def sb(name, shape, dtp):
            return es.enter_context(nc.sbuf_tensor("cv_" + name, shape, dtp))
        hT = load_hT(nc, S, es, Dm)
        ZT = sb("ZT", [128, 8, NTL], BF16)
        upad = sb("upad", [128, NLB, 130], F32)
        csb = sb("csb", [128, NTL + NHALO], F32)
        tcv = sb("tcv", [128, NLB, 128], F32)
        cw = sb("cw", [128, 8, 3], F32)
        wco = sb("wco", [128, 8, 1024], BF16)
        wb = [sb(f"w{i}", [128, 8, 128], BF16) for i in range(4)]
        stg = [sb(f"stg{i}", [128, 512], BF16) for i in range(4)]
        sgb = [sb(f"sgb{i}", [128, 512], BF16) for i in range(2)]
        rot = PSRot(range(8))
        win = Dm['win']
        S.dma('sp', cw[:], Dm['convw'][:, :, :], writes=[('cw',)], key=('cw',))
        for oc in range(8):
            S.dma('pool', wco[:, :, oc * 128:(oc + 1) * 128], Dm['wco'][oc], writes=[('wco', oc)], key=('wco', oc % 2))
        nw = 0

        def loadw(ch):
            nonlocal nw
            k = nw % 4
            nw += 1
            S.dma('pool', wb[k][:], win[ch], writes=[('cvw', k)], key=('cvw', k))
            return k
        for ch in range(8):
            kC = loadw(CH_C + ch)
            kX = loadw(CH_X + ch)
            kB = loadw(CH_B + ch)
            for (c0, n) in TOK_TILES:
                pi = rot.next()
                proj_fm(S, C, hT, wb[kC], ('cvw', kC), c0, n, pi)
                S.op('act', lambda e: e.activation(out=csb[:, c0:c0 + n], in_=C.PS[pi][:, 0:n], func=AF.Copy),
                     reads=[('ps', pi)], writes=[('csb', c0)])
            for (c0, n) in TOK_TILES:
                pi = rot.next()
                proj_fm(S, C, hT, wb[kX], ('cvw', kX), c0, n, pi)
                if n == 512:
                    r0 = c0 // 128
                    outv = upad[:, r0:r0 + 4, 2:130]
                    in0 = csb[:, c0:c0 + n].rearrange("p (r t) -> p r t", t=128)
                    in1 = C.PS[pi][:, 0:n].rearrange("p (r t) -> p r t", t=128)
                else:
                    outv = upad[:, :, 0:2]
                    in0 = csb[:, c0:c0 + n].rearrange("p (r t) -> p r t", t=2)
                    in1 = C.PS[pi][:, 0:n].rearrange("p (r t) -> p r t", t=2)
                S.op('dve', lambda e: e.tensor_tensor(out=outv, in0=in0, in1=in1, op=ALU.mult),
                     reads=[('ps', pi), ('csb', c0)], writes=[('upad', c0)])
            ukeys = [('upad', c0) for (c0, n) in TOK_TILES]
            S.op('dve', lambda e: e.tensor_scalar(out=tcv[:], in0=upad[:, :, 0:128], scalar1=cw[:, ch, 0:1], scalar2=None, op0=ALU.mult),
                 reads=ukeys + [('cw',)], writes=[('tcv',)])
            S.op('dve', lambda e: e.scalar_tensor_tensor(out=tcv[:], in0=upad[:, :, 1:129], scalar=cw[:, ch, 1:2], in1=tcv[:], op0=ALU.mult, op1=ALU.add),
                 reads=ukeys + [('tcv',)], writes=[('tcv',)])
            S.op('dve', lambda e: e.scalar_tensor_tensor(out=tcv[:], in0=upad[:, :, 2:130], scalar=cw[:, ch, 2:3], in1=tcv[:], op0=ALU.mult, op1=ALU.add),
                 reads=ukeys + [('tcv',)], writes=[('tcv',)])
            for (c0, n) in TOK_TILES[:4]:
                pi = rot.next()
                proj_fm(S, C, hT, wb[kB], ('cvw', kB), c0, n, pi)
                S.op('dve', lambda e: e.tensor_tensor(out=ZT[:, ch, c0:c0 + n], in0=tcv[:].rearrange("p r t -> p (r t)")[:, c0:c0 + n],
                                                      in1=C.PS[pi][:, 0:n], op=ALU.mult),
                     reads=[('ps', pi), ('tcv',)], writes=[('ZT', ch, c0)])
        ns = 0
        for oc in range(8):
            kGB = loadw(CH_GB + oc)
            kGA = loadw(CH_GA + oc)
            for (c0, n) in TOK_TILES[:4]:
                pg = rot.next()
                proj_fm(S, C, hT, wb[kGB], ('cvw', kGB), c0, n, pg)
                sk = ns % 2
                S.op('act', lambda e: e.activation(out=sgb[sk][:, 0:n], in_=C.PS[pg][:, 0:n], func=AF.Sigmoid),
                     reads=[('ps', pg)], writes=[('sgb', sk)])
                py = rot.next()
                for kc in range(8):
                    mm(S, C.PS[py][:, 0:n], wco[:, kc, oc * 128:(oc + 1) * 128], ZT[:, kc, c0:c0 + n], kc == 0, kc == 7,
                       reads=[('wco', oc)] + [('ZT', kc, c0)], writes=[('ps', py)])
                k = ns % 4
                S.op('dve', lambda e: e.tensor_tensor(out=stg[k][:, 0:n], in0=sgb[sk][:, 0:n], in1=C.PS[py][:, 0:n], op=ALU.mult),
                     reads=[('ps', py), ('sgb', sk)], writes=[('stg', k)])
                S.dma('sp', Dm['mb'][:, oc, c0:c0 + n], stg[k][:, 0:n], reads=[('stg', k)], writes=[], key=('stgst', k))
                ns += 1
                pa = rot.next()
                proj_fm(S, C, hT, wb[kGA], ('cvw', kGA), c0, n, pa)
                k = ns % 4
                S.op('act', lambda e: e.activation(out=stg[k][:, 0:n], in_=C.PS[pa][:, 0:n], func=AF.Sigmoid),
                     reads=[('ps', pa)], writes=[('stg', k)])
                S.dma('sp', Dm['sga'][:, oc, c0:c0 + n], stg[k][:, 0:n], reads=[('stg', k)], writes=[], key=('stgst', k))
                ns += 1
        S.barrier()


def phase_cmp(nc, S, C, Dm, A):
    with ExitStack() as es:
        def sb(name, shape, dtp):
            return es.enter_context(nc.sbuf_tensor("cm_" + name, shape, dtp))
        KC = [sb(f"kc2s{i}", [128, 17408], BF16) for i in range(2)]
        w1 = [sb(f"w1_{k}", [128, 16, 256], BF16) for k in range(2)]
        w2k = sb("w2k", [128, 2, 2, 128], BF16)
        w2v = sb("w2v", [128, 2, 64], BF16)
        pe = sb("pe", [128, 2, 16], BF16)
        bias = sb("bias", [128, 2, 2, 16], F32)
        G = sb("G", [128, 2, 2, 2, 1024], BF16)
        xs_ = [sb(f"gx{i}", [128, 512], F32) for i in range(2)]
        t1 = [sb(f"gt{i}", [128, 512], F32) for i in range(2)]
        sg = [sb(f"gs{i}", [128, 512], F32) for i in range(2)]
        rot = PSRot(range(8))
        xg = Dm['xg']
        for k in range(2):
            S.dma('pool', w1[k][:], Dm['w1'][k], writes=[('w1', k)], key=('w1', k))
            S.dma('pool', pe[:, k, :], Dm['pecol'][k], writes=[('pe', k)], key=('pe', k))
        for g in range(2):
            S.dma('pool', w2k[:, g, :, :], Dm['w2kpad'][g], writes=[('w2k', g)], key=('w2k', g))
        S.dma('pool', w2v[:], Dm['w2v'][:, :, :], writes=[('w2v',)], key=('w2v',))
        for k in range(2):
            for hh in range(2):
                pi = rot.next()
                for j in range(16):
                    mm(S, C.PS[pi][:, 0:1], w1[k][:, j, hh * 128:(hh + 1) * 128], pe[:, k, j:j + 1], j == 0, j == 15,
                       reads=[('w1', k), ('pe', k)], writes=[('ps', pi)])
                S.op('dve', lambda e: e.tensor_copy(out=bias[:, k, hh, 0:1], in_=C.PS[pi][:, 0:1]), reads=[('ps', pi)], writes=[('bias', k, hh)])
        n = 0
        for k in range(2):
            for g in range(2):
                b = n % 2
                n += 1
                kc = KC[b]
                S.op('pool', lambda e: e.memset(kc[:, 0:1], 0.0), writes=[('KC', b, 'z0')])
                S.op('pool', lambda e: e.memset(kc[:, 16385:16448], 0.0), writes=[('KC', b, 'z1')])
                S.op('pool', lambda e: e.memset(kc[64:128, 16384:16385], 0.0), writes=[('KC', b, 'z2')])
                for cc in range(NCORES):
                    row0 = cc * 768 + k * 128 + g * 64
                    src = xg[row0:row0 + 64, :].rearrange("d (r p) -> d r p", p=128)
                    lo = kc[0:64, 1 + cc * 128:1 + cc * 128 + 16 * 1024].rearrange("d (r q) -> d r q", q=1024)[:, :, 0:128]
                    hi = kc[64:128, cc * 128:cc * 128 + 16 * 1024].rearrange("d (r q) -> d r q", q=1024)[:, :, 0:128]
                    S.dma('sp', lo, src, reads=[('xg',)], writes=[('KC', b, 'lo', cc)], key=('KCl', b))
                    S.dma('sp', hi, src, reads=[('xg',)], writes=[('KC', b, 'hi', cc)], key=('KCl', b))
                kckeys = [('KC', b, 'z0'), ('KC', b, 'z1'), ('KC', b, 'z2')] + [('KC', b, h_, cc) for h_ in ('lo', 'hi') for cc in range(NCORES)]
                for hh in range(2):
                    for nt in range(2):
                        pi = rot.next()
                        for j in range(16):
                            st = 1 + 16 * (nt * 512 + (1 if j >= 8 else 0)) + 2 * (j % 8)
                            mm(S, C.PS[pi][:, :], w1[k][:, j, hh * 128:(hh + 1) * 128], kc[:, st:st + 16 * 512:16], j == 0, j == 15,
                               reads=[('w1', k)] + kckeys, writes=[('ps', pi)])
                        q = (hh * 2 + nt) % 2
                        S.op('act', lambda e: e.activation(out=xs_[q][:], in_=C.PS[pi][:, :], func=AF.Identity, bias=bias[:, k, hh, 0:1]),
                             reads=[('ps', pi), ('bias', k, hh)], writes=[('gx', q)])
                        S.op('dve', lambda e: e.tensor_tensor(out=t1[q][:], in0=xs_[q][:], in1=xs_[q][:], op=ALU.mult), reads=[('gx', q)], writes=[('gt', q)])
                        S.op('dve', lambda e: e.tensor_scalar(out=t1[q][:], in0=t1[q][:], scalar1=0.044715, scalar2=1.0, op0=ALU.mult, op1=ALU.add),
                             reads=[('gt', q)], writes=[('gt', q)])
                        S.op('dve', lambda e: e.tensor_tensor(out=t1[q][:], in0=t1[q][:], in1=xs_[q][:], op=ALU.mult), reads=[('gt', q), ('gx', q)], writes=[('gt', q)])
                        S.op('act', lambda e: e.activation(out=sg[q][:], in_=t1[q][:], func=AF.Sigmoid, scale=1.5957691216057308),
                             reads=[('gt', q)], writes=[('gs', q)])
                        S.op('dve', lambda e: e.tensor_tensor(out=G[:, k, g, hh, nt * 512:(nt + 1) * 512], in0=sg[q][:], in1=xs_[q][:], op=ALU.mult),
                             reads=[('gs', q), ('gx', q)], writes=[('G', k, g, hh, nt)])
        for nt in range(2):
            pi = rot.next()
            i = 0
            for g in range(2):
                for hh in range(2):
                    mm(S, C.PS[pi][:, :], w2k[:, g, hh, :], G[:, 0, g, hh, nt * 512:(nt + 1) * 512], i == 0, i == 3,
                       reads=[('w2k', g), ('G', 0, g, hh, nt)], writes=[('ps', pi)])
                    i += 1
            S.op('act', lambda e: e.activation(out=A.kcT[:, nt * 512:(nt + 1) * 512], in_=C.PS[pi][:, :], func=AF.Copy),
                 reads=[('ps', pi)], writes=[('kcT', nt)])
        S.op('pool', lambda e: e.memset(A.vc[:, :, :, 64:65], 1.0), writes=[('vc1',)])
        for T in range(8):
            pi = rot.next()
            for g in range(2):
                for hh in range(2):
                    mm(S, C.PS[pi][:, g * 64:(g + 1) * 64], G[:, 1, g, hh, T * 128:(T + 1) * 128], w2v[:, hh, :], hh == 0 and g == 0, hh == 1,
                       reads=[('w2v',), ('G', 1, g, hh, T // 4)], writes=[('ps', pi)], skip_group_check=True)
            S.op('dve', lambda e: e.tensor_copy(out=A.vc[:, T, :, 0:64], in_=C.PS[pi][:, 0:128].rearrange("p (g d) -> p g d", d=64)),
                 reads=[('ps', pi), ('vc1',)], writes=[('vc', T)])
        S.barrier()


def attn_branch(S, C, A, g, r, tiles, gate_col, first, tagb):
    nt = len(tiles)
    qv = A.qz[A.qslot][g][:, :, :]

    def mask_group(i0):
        bank = 6 + (i0 // 8) % 2
        pst = C.PS[bank][:].bitcast(BF16)
        for j in range(i0, min(i0 + 8, nt)):
            src = tiles[j][1][1]
            S.op('pe', lambda e: e.transpose(out=pst[:, (j - i0) * 128:(j - i0 + 1) * 128], in_=src, identity=C.ident[:, :]),
                 reads=list(tiles[j][3]) + [('ident',)], writes=[('ps', bank)])

    def stageA(i):
        kT, mk, va, rk = tiles[i]
        add = mk is not None and mk[0] == 'add'
        mul = mk is not None and mk[0] == 'mul'
        if mul and i % 8 == 0:
            mask_group(i)
        for hh in range(2):
            pi = (i % 2) * 2 + hh
            mm(S, C.PS[pi][:, :], kT, qv[:, hh * 4:(hh + 1) * 4, :], True, not add,
               reads=list(rk) + [('qt', A.qslot, g)], writes=[('ps', pi)])
            if add:
                mm(S, C.PS[pi][:, :], mk[1], A.I4[:, :], False, True, reads=list(rk) + [('I4',)], writes=[('ps', pi)])
            S.op('act', lambda e: e.activation(out=A.et[pi][:, :], in_=C.PS[pi][:, :], func=AF.Exp, scale=0.125),
                 reads=[('ps', pi)], writes=[('et', pi)])
            if mul:
                bank = 6 + (i // 8) % 2
                mT = C.PS[bank][:].bitcast(BF16)[:, (i % 8) * 128:(i % 8 + 1) * 128]
                et3 = A.et[pi][:, :].rearrange("p (h q) -> p h q", q=128)
                S.op('dve', lambda e: e.tensor_tensor(out=et3, in0=et3, in1=mT.unsqueeze(1).broadcast_to([128, 4, 128]), op=ALU.mult),
                     reads=[('et', pi), ('ps', bank)], writes=[('et', pi)])

    def stageB(i):
        kT, mk, va, rk = tiles[i]
        for hh in range(2):
            pi = (i % 2) * 2 + hh
            for h in range(4):
                mm(S, C.PS[4 + hh][:, h * 65:(h + 1) * 65], A.et[pi][:, h * 128:(h + 1) * 128], va, i == 0 and h == 0, i == nt - 1,
                   reads=list(rk) + [('et', pi)], writes=[('ps', 4 + hh)], skip_group_check=True)
    stageA(0)
    for i in range(nt):
        if i + 1 < nt:
            stageA(i + 1)
        stageB(i)
    for hh in range(2):
        Ov = C.PS[4 + hh][:, 0:260].rearrange("p (h e) -> p h e", e=65)
        dk = (tagb, 'den', hh)
        S.op('dve', lambda e: e.tensor_scalar(out=A.den[:, hh * 16:hh * 16 + 4], in0=Ov[:, :, 64], scalar1=1e-30, scalar2=None, op0=ALU.add),
             reads=[('ps', 4 + hh)], writes=[dk])
        S.op('dve', lambda e: e.reciprocal(out=A.den[:, hh * 16:hh * 16 + 4], in_=A.den[:, hh * 16:hh * 16 + 4]), reads=[dk], writes=[dk])
        gsl = C.g3[:, r, :].rearrange("p (h b) -> p h b", b=3)[:, g * 8 + hh * 4:g * 8 + hh * 4 + 4, gate_col]
        S.op('dve', lambda e: e.tensor_tensor(out=A.den[:, hh * 16:hh * 16 + 4], in0=A.den[:, hh * 16:hh * 16 + 4], in1=gsl, op=ALU.mult),
             reads=[dk, ('g3l',)], writes=[dk])
        fac = A.den[:, hh * 16:hh * 16 + 4].unsqueeze(2).broadcast_to([128, 4, 64])
        oa = A.oacc[:, hh * 4:(hh + 1) * 4, :]
        if first:
            S.op('dve', lambda e: e.tensor_tensor(out=oa, in0=Ov[:, :, 0:64], in1=fac, op=ALU.mult),
                 reads=[('ps', 4 + hh), dk], writes=[('oacc', hh)])
        else:
            S.op('dve', lambda e: e.tensor_tensor(out=A.otmp[:, hh * 4:(hh + 1) * 4, :], in0=Ov[:, :, 0:64], in1=fac, op=ALU.mult),
                 reads=[('ps', 4 + hh), dk], writes=[('otmp', hh)])
            S.op('pool', lambda e: e.tensor_tensor(out=oa, in0=oa, in1=A.otmp[:, hh * 4:(hh + 1) * 4, :], op=ALU.add),
                 reads=[('otmp', hh), ('oacc', hh)], writes=[('oacc', hh)])


def phase_attn(nc, S, C, Dm, A):
    with ExitStack() as es:
        def sb(name, shape, dtp):
            return es.enter_context(nc.sbuf_tensor("at_" + name, shape, dtp))
        xg = Dm['xg']
        ksT = sb("ksT", [128, NCORES, NTL], BF16)
        vs = sb("vs", [128, NCORES, NLB, 2, 65], BF16)
        nme = sb("nme", [128, SEQ], BF16)
        A.qz = [[sb(f"qz{i}_{g_}", [128, 8, 128], BF16) for g_ in range(2)] for i in range(2)]
        kw = [sb(f"kw{i}", [128, 12, 128], BF16) for i in range(2)]
        vw = [sb(f"vw{i}", [128, 12, 2, 65], BF16) for i in range(2)]
        A.et = [sb(f"et{i}", [128, 512], BF16) for i in range(4)]
        A.I4 = sb("I4", [128, 512], BF16)
        A.den = sb("den", [128, 32], F32)
        A.oacc = sb("oacc", [128, 8, 64], F32)
        A.otmp = sb("otmp", [128, 8, 64], F32)
        obf = sb("obf", [128, 512], BF16)
        oT = [sb(f"oT{i}", [128, 4, 128], BF16) for i in range(2)]
        Eh = [sb(f"Eh{i}", [128, 1024], F32) for i in range(2)]
        PC = sb("PC", [128, 1040], F32)
        psm = sb("psm", [128, 256], F32)
        wrk = sb("wrk", [128, 256], F32)
        Msel = sb("Msel", [128, 256], F32)
        mx = sb("mx", [128, 16], F32)
        rs = sb("rs", [128, 64], F32)
        ri = sb("ri", [128, 16], F32)
        cmx = sb("cmx", [128, 256], BF16)
        tblm = sb("tblm", [128, 32], F32)
        tbla = sb("tbla", [128, 32], F32)
        cblk = sb("cblk", [128, 32], F32)
        cneg = sb("cneg", [128, 1024], BF16)
        negw = sb("negw", [128, 1536], BF16)
        S.dma('pool', cmx[:], Dm['cmaskx'][:, :], writes=[('cmx',)], key=('cst', 0))
        S.dma('pool', cneg[:], Dm['causalneg'][:, :], writes=[('cneg',)], key=('cst', 1))
        S.dma('pool', negw[:], Dm['negw'][:, :], writes=[('negw',)], key=('cst', 2))
        S.dma('sp', tblm[:], Dm['tblm'][:, :], writes=[('tblm',)], key=('cst', 3))
        S.dma('sp', tbla[:], Dm['tbla'][:, :], writes=[('tbla',)], key=('cst', 4))
        S.dma('sp', cblk[:], Dm['cblk'][:, :], writes=[('cblk',)], key=('cst', 5))
        for h in range(4):
            S.op('pool', lambda e: e.tensor_copy(out=A.I4[:, h * 128:(h + 1) * 128], in_=C.ident[:, :]), reads=[('ident',)], writes=[('I4',)])
        S.op('pool', lambda e: e.memset(PC[:, 0:1], 0.0), writes=[('PC0',)])
        for i in range(2):
            for g_ in range(2):
                S.op('pool', lambda e: e.memset(A.qz[i][g_][(1 - g_) * 64:(2 - g_) * 64, :, :], 0.0), writes=[('qz0', i, g_)])
        S.op('pool', lambda e: e.memset(vs[:, :, :, :, 64:65], 1.0), writes=[('vs1',)])
        for i in range(2):
            S.op('pool', lambda e: e.memset(vw[i][:, :, :, 64:65], 1.0), writes=[('vw1', i)])
        for cc in range(NCORES):
            S.dma('sp', ksT[:, cc, :], xg[cc * 768 + 256:cc * 768 + 384, :], reads=[('xg',)], writes=[('ksT', cc)], key=('ksl', cc % 2))
            tmv = xg[cc * 768 + 512:cc * 768 + 768, :].rearrange("a (b c) -> (a b) c", c=256).rearrange("(u p) c -> p u c", p=128)
            for g in range(2):
                S.dma('sp', vs[:, cc, :, g, 0:64], tmv[:, :, g * 64:(g + 1) * 64], reads=[('xg',), ('vs1',)], writes=[('vs', cc, g)], key=('vsl', cc % 2))
        kv_keys = [('ksT', cc) for cc in range(NCORES)] + [('vs', cc, g_) for cc in range(NCORES) for g_ in range(2)]
        S.res[('g3l',)] = {'w': None, 'r': {}}
        nfin = 0
        for r in range(NLB):
            sl = r % 2
            A.qslot = sl
            for g_ in range(2):
                S.dma('sp', A.qz[sl][g_][g_ * 64:(g_ + 1) * 64, :, :], Dm['qT'][g_ * 64:(g_ + 1) * 64, :, r * 128:(r + 1) * 128],
                      reads=[('qz0', sl, g_)], writes=[('qt', sl, g_)], key=('qtl', sl))
            jj0 = 0 if r == 0 else -4
            wkeys = []
            if r > 0:
                srck = bass.AP(xg.tensor, ((4 * 768 + 384) * NTL + (r - 1) * 128), [[NTL, 128], [768 * NTL, 4], [1, 128]])
                S.dma('sp', kw[sl][:, 0:4, :], srck, reads=[('xg',)], writes=[('kw', sl, 0)], key=('kwl', sl))
                wkeys.append(('kw', sl, 0))
            srck = bass.AP(xg.tensor, (384 * NTL + r * 128), [[NTL, 128], [768 * NTL, 8], [1, 128]])
            S.dma('sp', kw[sl][:, 4:12, :], srck, reads=[('xg',)], writes=[('kw', sl, 1)], key=('kwl', sl))
            wkeys.append(('kw', sl, 1))
            for g in range(2):
                if r > 0:
                    srcv = bass.AP(xg.tensor, ((4 * 768 + 512) * NTL + ((r - 1) * 128) * 256 + 128 + g * 64), [[256, 128], [768 * NTL, 4], [1, 64]])
                    S.dma('sp', vw[sl][:, 0:4, g, 0:64], srcv, reads=[('xg',), ('vw1', sl)], writes=[('vw', sl, 0, g)], key=('vwl', sl))
                    wkeys.append(('vw', sl, 0, g))
                srcv = bass.AP(xg.tensor, (512 * NTL + (r * 128) * 256 + 128 + g * 64), [[256, 128], [768 * NTL, 8], [1, 64]])
                S.dma('sp', vw[sl][:, 4:12, g, 0:64], srcv, reads=[('xg',), ('vw1', sl)], writes=[('vw', sl, 1, g)], key=('vwl', sl))
                wkeys.append(('vw', sl, 1, g))
            nblk = 16 * (r + 1)
            n_c = 64 * (r + 1)
            for g in range(2):
                qv = A.qz[sl][g][:, :, :]
                for h in range(8):
                    eb = h % 2
                    nbank = 1 if n_c <= 512 else 2
                    for bk in range(nbank):
                        w = min(512, n_c - bk * 512)
                        mm(S, C.PS[6 + bk][:, 0:w], qv[:, h, :], A.kcT[:, bk * 512:bk * 512 + w], True, False,
                           reads=[('qt', sl, g), ('kcT', bk)], writes=[('ps', 6 + bk)])
                    pieces = [(n_c - 64, 128)] if r == 0 else [(n_c - 128, 64), (n_c - 64, 128)]
                    for (cst, mst) in pieces:
                        bk = cst // 512
                        mm(S, C.PS[6 + bk][:, cst - bk * 512:cst - bk * 512 + 64], C.ident[:, :], cmx[:, mst:mst + 64], False, True,
                           reads=[('ident',), ('cmx',)], writes=[('ps', 6 + bk)])
                    for bk in range(nbank):
                        w = min(512, n_c - bk * 512)
                        S.op('act', lambda e: e.activation(out=Eh[eb][:, bk * 512:bk * 512 + w], in_=C.PS[6 + bk][:, 0:w], func=AF.Exp, scale=0.125,
                                                           accum_out=rs[:, (h * 2 + bk) * 2:(h * 2 + bk) * 2 + 1]),
                             reads=[('ps', 6 + bk)], writes=[('Eh', eb, bk), ('rs', h, bk)])
                    rk = [('rs', h, 0)]
                    if nbank == 2:
                        S.op('dve', lambda e: e.tensor_tensor(out=ri[:, h:h + 1], in0=rs[:, h * 4:h * 4 + 1], in1=rs[:, h * 4 + 2:h * 4 + 3], op=ALU.add),
                             reads=[('rs', h, 0), ('rs', h, 1)], writes=[('ri', h)])
                        S.op('dve', lambda e: e.tensor_scalar(out=ri[:, h:h + 1], in0=ri[:, h:h + 1], scalar1=1e-30, scalar2=None, op0=ALU.add),
                             reads=[('ri', h)], writes=[('ri', h)])
                    else:
                        S.op('dve', lambda e: e.tensor_scalar(out=ri[:, h:h + 1], in0=rs[:, h * 4:h * 4 + 1], scalar1=1e-30, scalar2=None, op0=ALU.add),
                             reads=[('rs', h, 0)], writes=[('ri', h)])
                    S.op('dve', lambda e: e.reciprocal(out=ri[:, h:h + 1], in_=ri[:, h:h + 1]), reads=[('ri', h)], writes=[('ri', h)])
                    ehk = [('Eh', eb, bk) for bk in range(nbank)]
                    if h == 0:
                        S.op('dve', lambda e: e.tensor_scalar(out=PC[:, 1:1 + n_c], in0=Eh[eb][:, 0:n_c], scalar1=ri[:, h:h + 1], scalar2=None, op0=ALU.mult),
                             reads=ehk + [('ri', h)], writes=[('PC',)])
                    else:
                        S.op('dve', lambda e: e.scalar_tensor_tensor(out=PC[:, 1:1 + n_c], in0=Eh[eb][:, 0:n_c], scalar=ri[:, h:h + 1], in1=PC[:, 1:1 + n_c],
                                                                     op0=ALU.mult, op1=ALU.add),
                             reads=ehk + [('ri', h), ('PC',)], writes=[('PC',)])
                S.op('pool', lambda e: e.tensor_tensor(out=psm[:, 0:nblk], in0=PC[:, 0:4 * nblk:4], in1=PC[:, 1:1 + 4 * nblk:4], op=ALU.add),
                     reads=[('PC',), ('PC0',)], writes=[('psm',)])
                for k in range(2, 5):
                    S.op('pool', lambda e: e.tensor_tensor(out=psm[:, 0:nblk], in0=psm[:, 0:nblk], in1=PC[:, k:k + 4 * nblk:4], op=ALU.add),
                         reads=[('PC',), ('psm',)], writes=[('psm',)])
                nt_ = 16 if r == 0 else 32
                tb0 = 32 - nt_
                tail = psm[:, nblk - nt_:nblk]
                S.op('dve', lambda e: e.tensor_tensor(out=tail, in0=tail, in1=tblm[:, tb0:32], op=ALU.mult), reads=[('psm',), ('tblm',)], writes=[('psm',)])
                S.op('dve', lambda e: e.tensor_tensor(out=tail, in0=tail, in1=tbla[:, tb0:32], op=ALU.add), reads=[('psm',), ('tbla',)], writes=[('psm',)])
                S.op('dve', lambda e: e.memset(psm[:, 0:1], 1e9), reads=[('psm',)], writes=[('psm',)])
                S.op('dve', lambda e: e.max(out=mx[:, 0:8], in_=psm[:, 0:nblk]), reads=[('psm',)], writes=[('mx', 0)])
                S.op('dve', lambda e: e.match_replace(out=wrk[:, 0:nblk], in_to_replace=mx[:, 0:8], in_values=psm[:, 0:nblk], imm_value=-3.0),
                     reads=[('psm',), ('mx', 0)], writes=[('wrk',)])
                S.op('dve', lambda e: e.max(out=mx[:, 8:16], in_=wrk[:, 0:nblk]), reads=[('wrk',)], writes=[('mx', 1)])
                S.op('dve', lambda e: e.tensor_scalar(out=Msel[:, 0:nblk], in0=psm[:, 0:nblk], scalar1=mx[:, 15:16], scalar2=None, op0=ALU.is_ge),
                     reads=[('psm',), ('mx', 1)], writes=[('Msel',)])
                mt = Msel[:, nblk - nt_:nblk]
                S.op('dve', lambda e: e.tensor_tensor(out=mt, in0=mt, in1=cblk[:, tb0:32], op=ALU.mult), reads=[('Msel',), ('cblk',)], writes=[('Msel',)])
                S.op('pool', lambda e: e.tensor_copy(out=nme[:, 0:nblk * 64].rearrange("p (j k) -> p j k", k=64),
                                                     in_=Msel[:, 0:nblk].unsqueeze(2).broadcast_to([128, nblk, 64])),
                     reads=[('Msel',)], writes=[('nme',)])
                lastv = nme[:, nblk * 64 - 1024:nblk * 64]
                S.op('pool', lambda e: e.tensor_tensor(out=lastv, in0=lastv, in1=cneg[:, :], op=ALU.mult), reads=[('nme',), ('cneg',)], writes=[('nme',)])
                tiles = []
                for rp in range(r + 1):
                    for cc in range(NCORES):
                        B = 8 * rp + cc
                        tiles.append((ksT[:, cc, rp * 128:(rp + 1) * 128], ('mul', nme[:, B * 128:(B + 1) * 128]),
                                      vs[:, cc, rp, g, :], [('ksT', cc), ('vs', cc, g), ('nme',)]))
                attn_branch(S, C, A, g, r, tiles, 1, True, 'slc')
                Tl = (n_c - 1) // 128
                tiles = []
                for T in range(Tl + 1):
                    mk = None
                    if r % 2 == 1 and T == Tl:
                        mk = ('add', cmx[:, 64:192])
                    if r % 2 == 0 and T == Tl:
                        mk = ('add', cmx[:, 128:256])
                    if r % 2 == 0 and T == Tl - 1:
                        mk = ('add', cmx[:, 0:128])
                    tiles.append((A.kcT[:, T * 128:(T + 1) * 128], mk, A.vc[:, T, g, :], [('kcT', T // 4), ('vc', T), ('cmx',)]))
                attn_branch(S, C, A, g, r, tiles, 0, False, 'cmp')
                tiles = []
                for jj in range(jj0, 8):
                    tiles.append((kw[sl][:, jj + 4, :], ('add', negw[:, (jj + 4) * 128:(jj + 5) * 128]), vw[sl][:, jj + 4, g, :],
                                  wkeys + [('negw',)]))
                attn_branch(S, C, A, g, r, tiles, 2, False, 'win')
                S.op('act', lambda e: e.activation(out=obf[:, :], in_=A.oacc[:].rearrange("p h d -> p (h d)"), func=AF.Copy),
                     reads=[('oacc', 0), ('oacc', 1)], writes=[('obf',)])
                pst = C.PS[6][:].bitcast(BF16)
                for i in range(4):
                    S.op('pe', lambda e: e.transpose(out=pst[:, i * 128:(i + 1) * 128], in_=obf[:, i * 128:(i + 1) * 128], identity=C.ident[:, :]),
                         reads=[('obf',), ('ident',)], writes=[('ps', 6)])
                fk = nfin % 2
                nfin += 1
                S.op('dve', lambda e: e.tensor_copy(out=oT[fk][:].rearrange("p i t -> p (i t)"), in_=pst[:, 0:512]), reads=[('ps', 6)], writes=[('oT', fk)])
                S.dma('sp', Dm['onT'][:, g * 4:(g + 1) * 4, r * 128:(r + 1) * 128], oT[fk][:], reads=[('oT', fk)], writes=[], key=('oTst', fk))
        S.barrier()


def phase_merge(nc, S, C, Dm):
    with ExitStack() as es:
        def sb(name, shape, dtp):
            return es.enter_context(nc.sbuf_tensor("mg_" + name, shape, dtp))
        wno = sb("wno", [128, 8, 1024], BF16)
        wout = sb("wout", [128, 8, 1024], BF16)
        onT = [sb(f"onT{i}", [128, 8, 512], BF16) for i in range(2)]
        sga = [sb(f"sga{i}", [128, 8, 512], BF16) for i in range(2)]
        mb = [sb(f"mb{i}", [128, 8, 512], BF16) for i in range(2)]
        mg = [sb(f"mg{i}", [128, 8, 512], BF16) for i in range(2)]
        tmp = [sb(f"tmp{i}", [128, 512], F32) for i in range(2)]
        x1t = [sb(f"x1t{i}", [128, 1024], F32) for i in range(2)]
        rot = PSRot(range(8))
        for oc in range(8):
            S.dma('pool', wno[:, :, oc * 128:(oc + 1) * 128], Dm['wno'][oc], writes=[('wno', oc)], key=('wno', oc % 2))
        for kc in range(8):
            S.dma('pool', wout[:, kc, :], Dm['wout'][kc * 128:(kc + 1) * 128, :], writes=[('wout', kc)], key=('wout', kc % 2))
        nx = 0
        for ti, (c0, n) in enumerate(TOK_TILES[:4]):
            b = ti % 2
            S.dma('sp', onT[b][:], Dm['onT'][:, :, c0:c0 + n], writes=[('onT', b)], key=('onTl', b))
            S.dma('sp', sga[b][:], Dm['sga'][:, :, c0:c0 + n], writes=[('sga', b)], key=('sgal', b))
            S.dma('sp', mb[b][:], Dm['mb'][:, :, c0:c0 + n], writes=[('mb', b)], key=('mbl', b))
            for oc in range(8):
                pi = rot.next()
                for kc in range(8):
                    mm(S, C.PS[pi][:, :], wno[:, kc, oc * 128:(oc + 1) * 128], onT[b][:, kc, :], kc == 0, kc == 7,
                       reads=[('wno', oc), ('onT', b)], writes=[('ps', pi)])
                tk = oc % 2
                S.op('dve', lambda e: e.tensor_tensor(out=tmp[tk][:], in0=sga[b][:, oc, :], in1=C.PS[pi][:, :], op=ALU.mult),
                     reads=[('ps', pi), ('sga', b)], writes=[('tmp', tk)])
                S.op('pool', lambda e: e.tensor_tensor(out=mg[b][:, oc, :], in0=tmp[tk][:], in1=mb[b][:, oc, :], op=ALU.add),
                     reads=[('tmp', tk), ('mb', b)], writes=[('mg', b, oc)])
            for u4 in range(4):
                u = ti * 4 + u4
                xk = nx % 2
                nx += 1
                S.dma('sp', x1t[xk][:], Dm['x1'][u * 128:(u + 1) * 128, :], writes=[('x1t', xk)], key=('x1l', xk))
                for nh in range(2):
                    pi = rot.next()
                    for oc in range(8):
                        mm(S, C.PS[pi][:, :], mg[b][:, oc, u4 * 128:(u4 + 1) * 128], wout[:, oc, nh * 512:(nh + 1) * 512], oc == 0, oc == 7,
                           reads=[('wout', oc), ('mg', b, oc)], writes=[('ps', pi)])
                    S.op('dve', lambda e: e.tensor_tensor(out=x1t[xk][:, nh * 512:(nh + 1) * 512], in0=x1t[xk][:, nh * 512:(nh + 1) * 512],
                                                          in1=C.PS[pi][:, :], op=ALU.add),
                         reads=[('ps', pi), ('x1t', xk)], writes=[('x1t', xk)])
                S.dma('sp', Dm['x2'][u * 128:(u + 1) * 128, :], x1t[xk][:], reads=[('x1t', xk)], writes=[], key=('x2st', xk))
        S.barrier()


def build_program(stop_after=None, debug=()):
    nc = bass.Bass("TRN2", target_bir_lowering=False)
    dt = nc.dram_tensor
    Dm = {'_debug': debug}

    def din(name, shape, dtype=F32):
        Dm[name] = dt(name, list(shape), dtype, kind="ExternalInput").ap()

    def dscr(name, shape, dtype):
        if name in debug:
            t = dt("dbg_" + name, list(shape), dtype, kind="ExternalOutput")
        else:
            t = dt("scr_" + name, list(shape), dtype)
        Dm[name + "_t"] = t
        Dm[name] = t.ap()

    din("x", [NTL + NHALO, D])
    din("ident", [128, 128])
    din("g1", [128, 8])
    din("gm", [128, 8])
    din("wg1", [NFC, 128, 8, 128])
    din("wu1", [NFC, 128, 8, 128])
    din("wd1", [DFF, D])
    din("win", [NWCH, 128, 8, 128])
    din("convw", [128, 8, 3])
    din("wco", [8, 128, 8, 128])
    dscr("x1", [NTL + NHALO, D], F32)
    if 'hT_in' in debug:
        din("hT", [128, 17, 8, 128], BF16)
    else:
        dscr("hT", [128, 17, 8, 128], BF16)
    dscr("qT", [128, 8, NTL], BF16)
    dscr("mb", [128, 8, NTL], BF16)
    dscr("sga", [128, 8, NTL], BF16)
    din("w1", [2, 128, 16, 256])
    din("pecol", [2, 128, 16])
    din("w2kpad", [2, 128, 2, 128])
    din("w2v", [128, 2, 64])
    din("cmaskx", [128, 256])
    din("causalneg", [128, 1024])
    din("negw", [128, 1536])
    din("tblm", [128, 32])
    din("tbla", [128, 32])
    din("cblk", [128, 32])
    din("wno", [8, 128, 8, 128])
    din("wout", [D, D])
    din("g2", [128, 8])
    din("wg2", [NFC, 128, 8, 128])
    din("wu2", [NFC, 128, 8, 128])
    din("wd2", [DFF, D])
    din("gfin", [128, D])
    dscr("onT", [128, 8, NTL], BF16)
    dscr("x2", [NTL, D], F32)
    dscr("kcT", [128, 1024], BF16)
    dscr("vc", [128, 8, 2, 65], BF16)
    Dm['y'] = dt("y", [NTL, D], F32, kind="ExternalOutput").ap()
    Dm['xb_in_t'] = dt("scr_xb_in", [768, NTL], BF16)
    Dm['xb_in'] = Dm['xb_in_t'].ap()
    Dm['xg_t'] = dt("scr_xg", [NCORES * 768, NTL], BF16)
    Dm['xg'] = Dm['xg_t'].ap()
    if 'xg' in debug:
        Dm['dbg_xg'] = dt("dbg_xg", [NCORES * 768, NTL], BF16, kind="ExternalOutput").ap()
    if 'g3' in debug:
        Dm['dbg_g3'] = dt("dbg_g3", [128, NLB, 48], F32, kind="ExternalOutput").ap()

    with ExitStack() as es:
        block = es.enter_context(nc.Block())
        S = Sched(nc, es)
        C = Ctx()

        def sbg(name, shape, dtp):
            return es.enter_context(nc.sbuf_tensor("sb_" + name, shape, dtp))
        C.PS = [es.enter_context(nc.psum_tensor(f"ps{i}", [128, 512], F32)) for i in range(8)]
        C.ident = sbg("ident", [128, 128], BF16)
        C.junk = sbg("junk", [128, 1024], BF16)
        C.eps_t = sbg("eps_t", [128, 16], F32)
        C.ssq = sbg("ssq", [128, 16 * 16], F32)
        C.sq = sbg("sq", [128, 16 * 16], F32)
        C.rstd = sbg("rstd", [128, 16 * 16], F32)
        C.g3 = sbg("g3", [128, NLB, 48], F32)

        @block.gpsimd
        def _(_g):
            S.dma('pool', C.ident[:], Dm['ident'][:, :], writes=[('ident',)], key=('ident',))
            S.op('dve', lambda e: e.memset(C.eps_t[:], EPS), writes=[('eps',)])
            units = [(r * 128, 128) for r in range(NLB)] + [(NTL, NHALO)]

            def finish():
                if 'xg' in debug:
                    for i in range(24):
                        S.dma('sp', Dm['dbg_xg'][i * 256:(i + 1) * 256, :], Dm['xg'][i * 256:(i + 1) * 256, :], reads=[('xg',)], writes=[], key=('dbgxg',))
                if 'g3' in debug:
                    S.dma('sp', Dm['dbg_g3'][:, :, :], C.g3[:], reads=[('g3', u) for u in range(NLB)], writes=[], key=('dbgg3',))
                S.final_wait('sp')

            def epi1(S_, C_, P, ui, unit, pys, xs, init=False, sb=None, state=None):
                if init:
                    st = Ctx()
                    st.x1t = [sb(f"x1t{i}", [128, 1024], F32) for i in range(2)]
                    st.hn = [sb(f"hn{i}", [128, 1024], BF16) for i in range(2)]
                    st.hT = [sb(f"hTt{i}", [128, 8, 128], BF16) for i in range(2)]
                    st.gm = sb("gm", [128, 8], F32)
                    st.gmbc = sb("gmbc", [128, 8, 128], F32)
                    st.n = 0
                    st.first = True
                    return st
                st = state
                if st.first:
                    st.first = False
                    S.dma('sp', st.gm[:], Dm['gm'][:, :], writes=[('gm',)], key=('gm',))
                    S.op('dve', lambda e: e.tensor_copy(out=st.gmbc[:], in_=st.gm[:].unsqueeze(2).broadcast_to([128, 8, 128])),
                         reads=[('gm',)], writes=[('gmbc',)])
                r0, nb = unit
                k = st.n % 2
                st.n += 1
                xs_ap, xs_key = xs
                for nh in range(2):
                    S.op('dve', lambda e, nh=nh: e.scalar_tensor_tensor(out=st.x1t[k][0:nb, nh * 512:(nh + 1) * 512],
                                                                        in0=C.PS[pys[nh]][0:nb, :], scalar=0.5,
                                                                        in1=xs_ap[:, nh * 512:(nh + 1) * 512], op0=ALU.mult, op1=ALU.add),
                         reads=[('ps', pys[nh]), xs_key], writes=[('x1t', k, nh)])
                x1k = [('x1t', k, 0), ('x1t', k, 1)]
                S.dma('sp', Dm['x1'][r0:r0 + nb, :], st.x1t[k][0:nb, :], reads=x1k, writes=[], key=('x1st', k))
                rms_prologue(S, C, st.x1t[k][0:nb, :], nb, 10 + k, st.hn[k], ('hn', k), x1k, ('epi1', k))
                transpose_unit(S, C, st.hn[k], ('hn', k), nb, 6 + k, st.hT[k][:, :, 0:nb], ('hTt', k), st.gmbc)
                S.dma('sp', Dm['hT'][:, ui, :, 0:nb], st.hT[k][:, :, 0:nb], reads=[('hTt', k)], writes=[], key=('hTst', k))

            if 'hT_in' not in debug:
                ffn_phase(nc, S, C, "f1", Dm['x'], units, Dm['g1'], Dm['wg1'], Dm['wu1'], Dm['wd1'], epi1)
            if stop_after == 'ffn1':
                return finish()
            phase_proj(nc, S, C, Dm)
            if stop_after == 'proj':
                return finish()
            phase_conv(nc, S, C, Dm)
            if stop_after == 'conv':
                return finish()
            with ExitStack() as es_att:
                A = Ctx()
                A.kcT = es_att.enter_context(nc.sbuf_tensor("A_kcT", [128, 1024], BF16))
                A.vc = es_att.enter_context(nc.sbuf_tensor("A_vc", [128, 8, 2, 65], BF16))
                phase_cmp(nc, S, C, Dm, A)
                if 'kcT' in debug:
                    S.dma('sp', Dm['kcT'][:, :], A.kcT[:], key=('dbgk', 0))
                    S.dma('sp', Dm['vc'][:, :, :, :], A.vc[:], key=('dbgk', 1))
                if stop_after == 'cmp':
                    return finish()
                phase_attn(nc, S, C, Dm, A)
            if stop_after == 'attn':
                return finish()
            phase_merge(nc, S, C, Dm)
            if stop_after == 'merge':
                return finish()

            def epi2(S_, C_, P, ui, unit, pys, xs, init=False, sb=None, state=None):
                if init:
                    st = Ctx()
                    st.x3 = [sb(f"x3t{i}", [128, 1024], F32) for i in range(2)]
                    st.yo = [sb(f"yo{i}", [128, 1024], F32) for i in range(2)]
                    st.gf = sb("gfin", [128, 1024], F32)
                    st.n = 0
                    st.first = True
                    return st
                st = state
                if st.first:
                    st.first = False
                    S.dma('sp', st.gf[:], Dm['gfin'][:, :], writes=[('gfin',)], key=('gfin',))
                r0, nb = unit
                k = st.n % 2
                st.n += 1
                xs_ap, xs_key = xs
                for nh in range(2):
                    S.op('dve', lambda e, nh=nh: e.scalar_tensor_tensor(out=st.x3[k][0:nb, nh * 512:(nh + 1) * 512],
                                                                        in0=C.PS[pys[nh]][0:nb, :], scalar=0.5,
                                                                        in1=xs_ap[:, nh * 512:(nh + 1) * 512], op0=ALU.mult, op1=ALU.add),
                         reads=[('ps', pys[nh]), xs_key], writes=[('x3t', k, nh)])
                x3k = [('x3t', k, 0), ('x3t', k, 1)]
                slot = 10 + k
                ssq_col = C.ssq[0:nb, slot * 16:slot * 16 + 1]
                sq_col = C.sq[0:nb, slot * 16:slot * 16 + 1]
                rstd_col = C.rstd[0:nb, slot * 16:slot * 16 + 1]
                S.op('act', lambda e: e.activation(out=C.junk[0:nb, :], in_=st.x3[k][0:nb, :], func=AF.Square, accum_out=ssq_col),
                     reads=x3k, writes=[('junk',), ('e2ssq', k)])
                S.op('act', lambda e: e.activation(out=sq_col, in_=ssq_col, func=AF.Sqrt, bias=C.eps_t[0:nb, 0:1], scale=1.0 / D),
                     reads=[('e2ssq', k), ('e2rstd', k)], writes=[('e2sq', k)])
                S.op('dve', lambda e: e.reciprocal(out=rstd_col, in_=sq_col), reads=[('e2sq', k)], writes=[('e2rstd', k)])
                S.op('dve', lambda e: e.scalar_tensor_tensor(out=st.yo[k][0:nb, :], in0=st.x3[k][0:nb, :], scalar=rstd_col, in1=st.gf[0:nb, :],
                                                             op0=ALU.mult, op1=ALU.mult),
                     reads=x3k + [('e2rstd', k), ('gfin',)], writes=[('yo', k)])
                S.dma('sp', Dm['y'][r0:r0 + nb, :], st.yo[k][0:nb, :], reads=[('yo', k)], writes=[], key=('yst', k))

            ffn_phase(nc, S, C, "f2", Dm['x2'], units[:NLB], Dm['g2'], Dm['wg2'], Dm['wu2'], Dm['wd2'], epi2)
            finish()
    print("sems", S.nsem, "waits", S.nwait, "counts", S.cnt)
    return nc


def relayout_kcf(W):
    K, n = W.shape
    return np.ascontiguousarray(W.reshape(K // 128, 128, n // 128, 128).transpose(2, 1, 0, 3))


def vec_pk(g):
    return np.ascontiguousarray(g.reshape(8, 128).T)


def core_tokens(c):
    r = np.arange(NLB)[:, None]
    p = np.arange(128)[None, :]
    return (128 * (8 * r + c) + p).reshape(-1)


def core_consts(c):
    p = np.arange(128)[:, None]
    trel = 128 * c + p
    crel = np.arange(-128, 128)[None, :]
    cmaskx = np.where(16 * crel + 31 <= trel, 0.0, NEG).astype(np.float32)
    jrel = np.arange(-16, 16)[None, :]
    cur = 2 * c + p // 64
    forced = (jrel == cur) | (jrel == cur - 1)
    causal = jrel <= cur
    tblm = np.where(forced, 0.0, np.where(causal, 1.0, 0.0)).astype(np.float32)
    tbla = np.where(forced, 1e9, np.where(causal, 0.0, -1.0)).astype(np.float32)
    cblk = causal.astype(np.float32)
    kpos = np.arange(1024)[None, :]
    causalneg = np.where(kpos <= trel, 1.0, 0.0).astype(np.float32)
    wpos = np.arange(-512, 1024)[None, :]
    diff = trel - wpos
    negw = np.where((diff >= 0) & (diff < 512), 0.0, NEG).astype(np.float32)
    return dict(cmaskx=cmaskx, tblm=tblm, tbla=tbla, cblk=cblk, causalneg=causalneg, negw=negw)


def prep_inputs(inp):
    f = lambda k: np.asarray(inp[k], np.float32)
    x = f("x")[0]
    w_in = f("w_in")[0]
    q = w_in[:, 0:1024]
    qperm = np.concatenate([np.concatenate([q[:, i * 64:(i + 1) * 64], q[:, (8 + i) * 64:(9 + i) * 64]], axis=1) for i in range(8)], axis=1)
    sec = lambda a, b: w_in[:, a:b]
    nsag = np.zeros((1024, 128), np.float32)
    nsag[:, :48] = sec(1792, 1840)
    wcat = np.concatenate([qperm, sec(1024, 1152), sec(1152, 1280), sec(1280, 1408), sec(1536, 1664),
                           sec(1408, 1536), sec(1664, 1792), nsag,
                           sec(1840, 2864), sec(2864, 3888), sec(3888, 4912), sec(4912, 5936), sec(5936, 6960)], axis=1)
    assert wcat.shape[1] == NWCH * 128
    common = {
        "ident": np.eye(128, dtype=np.float32),
        "g1": vec_pk(f("ffn1_norm")[0]),
        "gm": vec_pk(f("mix_norm")[0]),
        "wg1": relayout_kcf(f("ffn1_w_gate")[0]),
        "wu1": relayout_kcf(f("ffn1_w_up")[0]),
        "wd1": np.ascontiguousarray(f("ffn1_w_down")[0]),
        "win": relayout_kcf(wcat),
        "convw": np.ascontiguousarray(f("conv_w")[0].reshape(3, 8, 128).transpose(2, 1, 0)),
        "wco": relayout_kcf(f("w_conv_out")[0]),
        "w1": np.ascontiguousarray(np.stack([f("cmp_k_w1")[0], f("cmp_v_w1")[0]]).reshape(2, 16, 128, 256).transpose(0, 2, 1, 3)),
        "pecol": np.ascontiguousarray(np.stack([f("cmp_pe_k")[0], f("cmp_pe_v")[0]]).reshape(2, 16, 128).transpose(0, 2, 1)),
        "w2v": np.ascontiguousarray(f("cmp_v_w2")[0].reshape(2, 128, 64).transpose(1, 0, 2)),
        "wno": relayout_kcf(f("w_nsa_out")[0]),
        "wout": np.ascontiguousarray(f("w_out")[0]),
        "g2": vec_pk(f("ffn2_norm")[0]),
        "wg2": relayout_kcf(f("ffn2_w_gate")[0]),
        "wu2": relayout_kcf(f("ffn2_w_up")[0]),
        "wd2": np.ascontiguousarray(f("ffn2_w_down")[0]),
        "gfin": np.ascontiguousarray(np.broadcast_to(f("final_norm")[None, :], (128, D))),
    }
    w2k = f("cmp_k_w2")[0].reshape(2, 128, 64)
    w2kpad = np.zeros((2, 128, 2, 128), np.float32)
    for g in range(2):
        w2kpad[g, :, :, g * 64:(g + 1) * 64] = w2k.transpose(1, 0, 2)
    common["w2kpad"] = w2kpad
    maps = []
    for c in range(NCORES):
        tok = core_tokens(c)
        xc = np.zeros((NTL + NHALO, D), np.float32)
        xc[:NTL] = x[tok]
        for r in range(NLB):
            B = 8 * r + c
            if B > 0:
                xc[NTL + 2 * r: NTL + 2 * r + 2] = x[128 * B - 2:128 * B]
        m = dict(common)
        m["x"] = xc
        m.update(core_consts(c))
        maps.append(m)
    return maps


def kernel(**inputs):
    nc = build_program()
    maps = prep_inputs(inputs)
    res = run_bass_kernel_spmd(nc, maps, core_ids=list(range(NCORES)))
    out = np.zeros((1, SEQ, D), np.float32)
    for c in range(NCORES):
        out[0, core_tokens(c)] = res.results[c]["y"]
    return out
```
